# Optimizing a Trainium2 kernel written in Bass

```python
import jax, jax.numpy as jnp
from jax import lax
import numpy as np

D_MODEL = 1024
BATCH = 8
SEQ = 4096
DEPTH = 1

HEAD_DIM = 64
GRID_W = 64
NA_HEADS = 8
NA_WIN_ROWS = 8
NA_WIN_COLS = 16
NA_QBLOCK_COLS = 16
NA_KBLOCK_COLS = 32
DIL_GROUPS = ((128, 1), (512, 4), (2048, 16))
DIL_HEADS_PER_GROUP = 4
DIL_HEADS = DIL_HEADS_PER_GROUP * len(DIL_GROUPS)
NA_WIDTH = NA_HEADS * HEAD_DIM
DIL_WIDTH = DIL_HEADS * HEAD_DIM
DIL_OUT_WIDTH = DIL_HEADS_PER_GROUP * HEAD_DIM
IN_WIDTHS = (NA_WIDTH, NA_WIDTH, NA_WIDTH, DIL_WIDTH, DIL_WIDTH, DIL_WIDTH, D_MODEL, D_MODEL)
IN_WIDTH = sum(IN_WIDTHS)
D_FF = 4 * D_MODEL
PLE_DIM = 256
ROPE_THETA = 10000.0
RMS_EPS = 1e-6
NEG_INF = -1e30

kernel_name = "hybrid_na_dilated_gated_encoder"


def rms_norm(x, g):
    xf = x.astype(jnp.float32)
    y = xf * lax.rsqrt(jnp.mean(xf * xf, axis=-1, keepdims=True) + RMS_EPS)
    return (y * g.astype(jnp.float32)).astype(x.dtype)


def rotary(x, positions):
    half = HEAD_DIM // 2
    inv_freq = ROPE_THETA ** (-jnp.arange(half, dtype=jnp.float32) / half)
    ang = positions.astype(jnp.float32)[:, None, :, None] * inv_freq
    cos, sin = jnp.cos(ang), jnp.sin(ang)
    xf = x.astype(jnp.float32)
    x1, x2 = xf[..., :half], xf[..., half:]
    return jnp.concatenate([x1 * cos - x2 * sin, x2 * cos + x1 * sin], axis=-1).astype(x.dtype)


def neighborhood_attention(q, k, v, rpb):
    b, h, s, dh = q.shape
    rows = s // GRID_W
    wr = min(NA_WIN_ROWS, rows)
    n_cb = GRID_W // NA_QBLOCK_COLS
    r = np.arange(rows)
    rs = np.clip(r - wr // 2, 0, rows - wr)
    row_idx = rs[:, None] + np.arange(wr)[None, :]
    row_off = row_idx - r[:, None] + (NA_WIN_ROWS - 1)
    c = np.arange(GRID_W)
    cs = np.clip(c - NA_WIN_COLS // 2, 0, GRID_W - NA_WIN_COLS)
    cb = np.arange(n_cb)
    kc0 = np.clip(cb * NA_QBLOCK_COLS - NA_WIN_COLS // 2, 0, GRID_W - NA_KBLOCK_COLS)
    col_idx = kc0[:, None] + np.arange(NA_KBLOCK_COLS)[None, :]
    qcol = cb[:, None] * NA_QBLOCK_COLS + np.arange(NA_QBLOCK_COLS)[None, :]
    kcol = col_idx[:, None, :]
    qs = cs[qcol][:, :, None]
    col_valid = (kcol >= qs) & (kcol < qs + NA_WIN_COLS)
    col_off = np.clip(kcol - qcol[:, :, None], -(NA_WIN_COLS - 1), NA_WIN_COLS - 1) + (NA_WIN_COLS - 1)
    bias = jnp.take(rpb[:, row_off].astype(jnp.float32), col_off, axis=-1)
    bias = jnp.where(col_valid, bias, NEG_INF).transpose(0, 1, 3, 4, 2, 5)

    qg = (q * (dh ** -0.5)).reshape(b, h, rows, n_cb, NA_QBLOCK_COLS, dh)
    kc = jnp.take(k.reshape(b, h, rows, GRID_W, dh), col_idx, axis=3)
    vc = jnp.take(v.reshape(b, h, rows, GRID_W, dh), col_idx, axis=3)
    scores = jnp.stack(
        [jnp.einsum('bhrcqd,bhrckd->bhrcqk', qg, jnp.take(kc, row_idx[:, i], axis=2)).astype(jnp.float32)
         for i in range(wr)], axis=-2)
    scores = scores + bias[None]
    probs = jax.nn.softmax(scores.reshape(*scores.shape[:-2], wr * NA_KBLOCK_COLS), axis=-1)
    probs = probs.reshape(scores.shape).astype(v.dtype)
    out = jnp.einsum('bhrcqk,bhrckd->bhrcqd', probs[..., 0, :], jnp.take(vc, row_idx[:, 0], axis=2))
    for i in range(1, wr):
        out = out + jnp.einsum('bhrcqk,bhrckd->bhrcqd', probs[..., i, :], jnp.take(vc, row_idx[:, i], axis=2))
    return out.reshape(b, h, s, dh)


def banded_attention(q, k, v, radius):
    *lead, L, dh = q.shape
    blk = radius
    nb = -(-L // blk)
    lp = nb * blk
    nlead = len(lead)
    qb = jnp.pad(q, [(0, 0)] * nlead + [(0, lp - L), (0, 0)]).reshape(*lead, nb, blk, dh)

    def band(t):
        tb = jnp.pad(t, [(0, 0)] * nlead + [(blk, lp - L + blk), (0, 0)]).reshape(*lead, nb + 2, blk, dh)
        return jnp.concatenate([tb[..., :-2, :, :], tb[..., 1:-1, :, :], tb[..., 2:, :, :]], axis=-2)

    kb, vb = band(k), band(v)
    s = jnp.einsum('...nqd,...nkd->...nqk', qb, kb).astype(jnp.float32) * (dh ** -0.5)
    qi = np.arange(lp).reshape(nb, blk)[:, :, None]
    kj = np.arange(nb)[:, None, None] * blk - blk + np.arange(3 * blk)[None, None, :]
    valid = (kj >= 0) & (kj < L) & (np.abs(qi - kj) <= radius)
    s = jnp.where(valid, s, NEG_INF)
    lse = jax.nn.logsumexp(s, axis=-1)
    pr = jnp.exp(s - lse[..., None]).astype(v.dtype)
    o = jnp.einsum('...nqk,...nkd->...nqd', pr, vb)
    return o.reshape(*lead, lp, dh)[..., :L, :], lse.reshape(*lead, lp)[..., :L]


def dilated_attention(q, k, v):
    b, _, s, dh = q.shape
    hg = DIL_HEADS_PER_GROUP
    outs, lses = [], []
    for g, (window, dil) in enumerate(DIL_GROUPS):
        radius = window // (2 * dil)

        def split(t):
            return t[:, g * hg:(g + 1) * hg].reshape(b, hg, s // dil, dil, dh).swapaxes(2, 3)

        o, lse = banded_attention(split(q), split(k), split(v), radius)
        outs.append(o.swapaxes(2, 3).reshape(b, hg, s, dh))
        lses.append(lse.swapaxes(2, 3).reshape(b, hg, s))
    w = jax.nn.softmax(jnp.stack(lses), axis=0).astype(q.dtype)
    return jnp.einsum('gbhs,gbhsd->bhsd', w, jnp.stack(outs))


def setup_inputs(seed: int = 0) -> dict:
    key = jax.random.key(seed)
    ks = jax.random.split(key, 20)
    f32 = jnp.float32

    def nrm(k, shape, fan_in):
        return jax.random.normal(k, shape, f32) * (fan_in ** -0.5)

    def gain(k, shape):
        return 1.0 + 0.01 * jax.random.normal(k, shape, f32)

    return {
        "x": jax.random.normal(ks[0], (BATCH, SEQ, D_MODEL), f32),
        "p": jax.random.normal(ks[1], (DEPTH, BATCH, SEQ, PLE_DIM), f32),
        "positions": (jnp.arange(SEQ, dtype=jnp.int32)[None, :]
                      + jax.random.randint(ks[2], (BATCH, 1), 0, 1024, dtype=jnp.int32)),
        "g_mix": gain(ks[3], (DEPTH, D_MODEL)),
        "w_in": nrm(ks[4], (DEPTH, D_MODEL, IN_WIDTH), D_MODEL),
        "rpb": 0.02 * jax.random.normal(ks[5], (DEPTH, NA_HEADS, 2 * NA_WIN_ROWS - 1, 2 * NA_WIN_COLS - 1), f32),
        "w_branch_na": nrm(ks[6], (DEPTH, NA_WIDTH, D_MODEL), NA_WIDTH),
        "w_branch_dil": nrm(ks[7], (DEPTH, DIL_OUT_WIDTH, D_MODEL), DIL_OUT_WIDTH),
        "w_out": nrm(ks[8], (DEPTH, D_MODEL, D_MODEL), D_MODEL),
        "g_mlp": gain(ks[9], (DEPTH, D_MODEL)),
        "w_up": nrm(ks[10], (DEPTH, D_MODEL, D_FF), D_MODEL),
        "w_down": nrm(ks[11], (DEPTH, D_FF, D_MODEL), D_FF),
        "g_ple": gain(ks[12], (DEPTH, D_MODEL)),
        "w_ple_gate": nrm(ks[13], (DEPTH, D_MODEL, D_MODEL), D_MODEL),
        "w_ple_proj": nrm(ks[14], (DEPTH, PLE_DIM, D_MODEL), PLE_DIM),
        "g_final": gain(ks[15], (D_MODEL,)),
    }


def reference(x, p, positions, g_mix, w_in, rpb, w_branch_na, w_branch_dil, w_out, g_mlp, w_up, w_down,
              g_ple, w_ple_gate, w_ple_proj, g_final):
    b, s, _ = x.shape
    split_points = [int(v) for v in np.cumsum(IN_WIDTHS)[:-1]]

    def heads(t, n):
        return t.reshape(b, s, n, HEAD_DIM).transpose(0, 2, 1, 3)

    def merge(t):
        return t.transpose(0, 2, 1, 3).reshape(b, s, -1)

    h = x
    for i in range(DEPTH):
        a = rms_norm(h, g_mix[i])
        qa, ka, va, qd, kd, vd, gate_na, gate_dil = jnp.split(a @ w_in[i], split_points, axis=-1)
        y_na = merge(neighborhood_attention(heads(qa, NA_HEADS), heads(ka, NA_HEADS), heads(va, NA_HEADS), rpb[i]))
        y_dil = merge(dilated_attention(rotary(heads(qd, DIL_HEADS), positions),
                                        rotary(heads(kd, DIL_HEADS), positions),
                                        heads(vd, DIL_HEADS)))
        mixed = (jax.nn.sigmoid(gate_na) * (y_na @ w_branch_na[i])
                 + jax.nn.sigmoid(gate_dil) * (y_dil @ w_branch_dil[i]))
        h = h + mixed @ w_out[i]
        c = rms_norm(h, g_mlp[i])
        h = h + jnp.square(jax.nn.relu(c @ w_up[i])) @ w_down[i]
        e = rms_norm(h, g_ple[i])
        h = h + jax.nn.sigmoid(e @ w_ple_gate[i]) * (p[i] @ w_ple_proj[i])
    return rms_norm(h, g_final)
```

```python
import numpy as np
import ml_dtypes
from contextlib import ExitStack
import concourse.bass as bass
import concourse.mybir as mybir
from concourse.bass_utils import run_bass_kernel_spmd

F32 = mybir.dt.float32
BF16 = mybir.dt.bfloat16
I32 = mybir.dt.int32
ALU = mybir.AluOpType
AF = mybir.ActivationFunctionType

SEQ = 4096
D = 1024
NT = SEQ // 128
NEG = -30000.0
PI_SAFE = 3.1415925
TWO_PI = 6.283185307179586
CW1 = 6.28125
CW2 = TWO_PI - CW1
DILS = (1, 4, 16)
N_LOADS = 24
KEEP_WARM = False


class Sched:
    EPOCH = 30000

    def __init__(self, nc, es):
        self.nc = nc
        self.es = es
        self.eng = {"pe": nc.tensor, "act": nc.scalar, "dve": nc.vector,
                    "pool": nc.gpsimd, "sp": nc.sync}
        self.csem = {}
        self.cnt = {}
        self.nsem = 0
        for e in ("pe", "act", "dve", "pool"):
            self._new_epoch(e)
        self.dsem = {}
        self.dsem_by_id = {}
        self.seen = {e: {} for e in self.eng}
        self.lastw = {}
        self.readers = {}
        self.pending = {e: False for e in self.eng}
        self.n_inst = {e: 0 for e in self.eng}
        self.n_wait = {e: 0 for e in self.eng}
        self.all_sems = []

    def _new_sem(self, name):
        self.nsem += 1
        s = self.es.enter_context(self.nc.semaphore(f"{name}_{self.nsem}"))
        return s

    def _new_epoch(self, e):
        self.csem[e] = self._new_sem("c" + e)
        self.cnt[e] = 0

    def _deps(self, reads, writes):
        toks = []
        for r in reads:
            t = self.lastw.get(r)
            if t is not None:
                toks.append(t)
        for w in writes:
            t = self.lastw.get(w)
            if t is not None:
                toks.append(t)
            toks.extend(self.readers.get(w, {}).values())
        return toks

    def _emit_waits(self, e, toks):
        need = {}
        for (sem, val, src) in toks:
            if src == "pe" and e == "pe":
                continue
            k = id(sem)
            if k in self.dsem_by_id:
                val = max(val, self.dsem_by_id[k][1])
            if src in self.csem and sem is self.csem[src]:
                assert val <= self.cnt[src], f"wait on un-incremented op of {src}"
            if self.seen[e].get(k, 0) >= val:
                continue
            if k not in need or need[k][1] < val:
                need[k] = (sem, val)
        for k, (sem, val) in need.items():
            self.eng[e].wait_ge(sem, val)
            self.seen[e][k] = val
            self.n_wait[e] += 1

    def _record(self, tok, who, reads, writes):
        for w in writes:
            self.lastw[w] = tok
            self.readers[w] = {}
        for r in reads:
            if r in writes:
                continue
            self.readers.setdefault(r, {})[who] = tok

    def op(self, e, fn, reads=(), writes=(), inc=True):
        reads = tuple(reads)
        writes = tuple(writes)
        self._emit_waits(e, self._deps(reads, writes))
        inst = fn(self.eng[e])
        self.n_inst[e] += 1
        if self.cnt[e] >= self.EPOCH and not self.pending[e]:
            self._new_epoch(e)
        if inc:
            inst.then_inc(self.csem[e], 1)
            self.cnt[e] += 1
            tok = (self.csem[e], self.cnt[e], e)
            self.pending[e] = False
        else:
            tok = (self.csem[e], self.cnt[e] + 1, e)
            self.pending[e] = True
        self._record(tok, e, reads, writes)
        return inst

    def dma(self, q, out, in_, semkey, reads=(), writes=()):
        reads = tuple(reads)
        writes = tuple(writes)
        self._emit_waits(q, self._deps(reads, writes))
        if semkey not in self.dsem:
            self.dsem[semkey] = [self._new_sem("d"), 0]
            self.dsem_by_id[id(self.dsem[semkey][0])] = self.dsem[semkey]
        ent = self.dsem[semkey]
        inst = self.eng[q].dma_start(out=out, in_=in_)
        inst.then_inc(ent[0], 16)
        ent[1] += 16
        self.n_inst[q] += 1
        who = "dma:" + str(semkey)
        tok = (ent[0], ent[1], who)
        self._record(tok, who, reads, writes)
        return inst

    def barrier(self):
        toks = []
        for e in ("pe", "act", "dve", "pool"):
            assert not self.pending[e]
            if self.cnt[e] > 0:
                toks.append((self.csem[e], self.cnt[e], e))
        for k, ent in self.dsem.items():
            if ent[1] > 0:
                toks.append((ent[0], ent[1], "dma:" + str(k)))
        for e in self.eng:
            self._emit_waits(e, [t for t in toks if not (t[2] == e and e != "pe")] +
                             [t for t in toks if t[2] == e and e != "pe"])
        self.lastw = {}
        self.readers = {}

    def finish(self, keys):
        for k in keys:
            ent = self.dsem[k]
            self.nc.sync.wait_ge(ent[0], ent[1])
        for e, p in self.pending.items():
            assert not p


def _blk(w):
    K, N = w.shape
    return np.ascontiguousarray(w.reshape(K // 128, 128, N // 128, 128).transpose(2, 1, 0, 3))


def _prep_shared(inp):
    f32 = np.float32
    w_in = np.asarray(inp["w_in"][0], f32)
    sh = {}
    swap = np.concatenate([np.arange(32, 64), np.arange(0, 32)])
    wD = np.empty((6, 128, 8, 640), f32)
    for g in range(3):
        for P in range(2):
            h0 = 4 * g + 2 * P
            cq = 1536 + 64 * h0 + np.arange(128)
            ck = 2304 + 64 * h0 + np.arange(128)
            cv = 3072 + 64 * h0 + np.arange(128)
            sw = np.concatenate([swap, 64 + swap])
            cols = np.concatenate([cq, cq[sw], ck, ck[sw], cv])
            blk = w_in[:, cols].reshape(8, 128, 640).transpose(1, 0, 2)
            wD[2 * g + P] = blk
    sh["wD"] = wD.reshape(6, 128, 8 * 640)
    wN = np.empty((4, 128, 8, 384), f32)
    for pi in range(4):
        cq = 128 * pi + np.arange(128)
        cols = np.concatenate([cq, 512 + cq, 1024 + cq])
        wN[pi] = w_in[:, cols].reshape(8, 128, 384).transpose(1, 0, 2)
    sh["wN"] = wN.reshape(4, 128, 8 * 384)
    gna = _blk(w_in[:, 3840:4864])
    gdil = _blk(w_in[:, 4864:5888])
    wo = _blk(np.asarray(inp["w_out"][0], f32))
    wu = _blk(np.asarray(inp["w_up"][0], f32))
    wd = _blk(np.asarray(inp["w_down"][0], f32))
    wpg = _blk(np.asarray(inp["w_ple_gate"][0], f32))

    def four(b, i):
        return b[4 * i:4 * i + 4].transpose(1, 0, 2, 3).reshape(128, 4096)

    loads = [four(gna, 0), four(gdil, 0), four(gna, 1), four(gdil, 1), four(wo, 0), four(wo, 1)]
    loads += [four(wu, i) for i in range(8)]
    loads += [wd[c].reshape(128, 4096) for c in range(8)]
    loads += [four(wpg, 0), four(wpg, 1)]
    assert len(loads) == N_LOADS
    sh["wS"] = np.ascontiguousarray(np.stack(loads))
    def pk(w):
        K, N = w.shape
        return np.ascontiguousarray(w.reshape(K // 128, 128, N).transpose(1, 0, 2).reshape(128, -1))
    sh["wA"] = pk(np.asarray(inp["w_branch_na"][0], f32))
    sh["wB"] = pk(np.asarray(inp["w_branch_dil"][0], f32))
    sh["wPP"] = pk(np.asarray(inp["w_ple_proj"][0], f32))
    gs = [inp["g_mix"][0], inp["g_mlp"][0], inp["g_ple"][0], inp["g_final"]]
    sh["gcols"] = np.ascontiguousarray(
        np.stack([np.asarray(g, f32).reshape(8, 128).T for g in gs], axis=1).reshape(128, 32))
    rpb = np.asarray(inp["rpb"][0], f32)
    kc = np.arange(64)[:, None]
    qc = np.arange(64)[None, :]
    cs = np.clip(qc - 8, 0, 48)
    colvalid = (kc >= cs) & (kc < cs + 16)
    coff = np.clip(kc - qc, -15, 15) + 15
    tab = np.empty((8, 2, 2, 64, 22, 64), f32)
    for a in range(2):
        for m in range(22):
            dr = 10 + a - m
            inr = abs(dr) <= 7
            vals = rpb[:, min(max(dr, -7), 7) + 7][:, coff]
            allv = np.where(colvalid[None] & inr, vals, f32(NEG))
            midv = np.where(colvalid[None] & (dr >= -4) & (dr <= 3), vals, f32(NEG))
            tab[:, 0, a, :, m, :] = midv
            tab[:, 1, a, :, m, :] = allv
    sh["rpbx"] = np.ascontiguousarray(tab.reshape(8, 2, 128, 22 * 64))
    sh["ident_bf"] = np.eye(128, dtype=f32).astype(ml_dtypes.bfloat16)
    sh["ident_f"] = np.eye(128, dtype=f32)
    half = 32
    invf = (f32(10000.0) ** (-(np.arange(half, dtype=f32)) / f32(half))).astype(f32)
    cst = np.zeros((128, 4), f32)
    cst[:, 0] = invf[np.arange(128) % 32]
    cst[:, 1] = np.where((np.arange(128) % 64) < 32, -1.0, 1.0)
    cst[:, 2] = 1e-6
    cst[:, 3] = 1.0
    sh["cst"] = cst
    p = np.arange(128)[:, None]
    f = np.arange(128)[None, :]
    dm = np.stack([np.where(p >= f + 64, 0.0, NEG), np.where(np.abs(p - f) <= 64, 0.0, NEG),
                   np.where(p <= f - 64, 0.0, NEG)], axis=1).astype(f32)
    sh["dmask"] = np.ascontiguousarray(dm.reshape(128, 384))
    return sh


def build_program(debug=(), stop_after=None):
    nc = bass.Bass("TRN2", target_bir_lowering=False)

    def din(name, shape, dt):
        return nc.dram_tensor(name, list(shape), dt, kind="ExternalInput").ap()

    x_d = din("x", [SEQ, D], F32)
    p_d = din("p", [SEQ, 256], F32)
    pos_d = din("pos", [1, SEQ], I32)
    gcols_d = din("gcols", [128, 32], F32)
    cst_d = din("cst", [128, 4], F32)
    identb_d = din("ident_bf", [128, 128], BF16)
    identf_d = din("ident_f", [128, 128], F32)
    dmask_d = din("dmask", [128, 384], F32)
    wD_d = din("wD", [6, 128, 5120], F32)
    wN_d = din("wN", [4, 128, 3072], F32)
    wS_d = din("wS", [N_LOADS, 128, 4096], F32)
    wA_d = din("wA", [128, 4096], F32)
    wB_d = din("wB", [128, 2048], F32)
    wPP_d = din("wPP", [128, 2048], F32)
    rpbx_d = din("rpbx", [8, 2, 128, 1408], F32)
    out_d = nc.dram_tensor("out", [SEQ, D], F32, kind="ExternalOutput").ap()
    dbg_out = {}

    with ExitStack() as es:
        S = Sched(nc, es)

        def sb(stack, name, shape, dt):
            return stack.enter_context(nc.sbuf_tensor("s_" + name, list(shape), dt))

        banks = [es.enter_context(nc.psum_tensor(f"bank{i}", [128, 512], F32)) for i in range(8)]
        rot = {"pt": 0}

        def bank(pool):
            k = tuple(pool)
            i = rot.get(k, 0)
            rot[k] = i + 1
            b = pool[i % len(pool)]
            return banks[b], ("ps", b)

        identb = sb(es, "identb", [128, 128], BF16)
        identf = sb(es, "identf", [128, 128], F32)
        gcols = sb(es, "gcols", [128, 32], F32)
        cst = sb(es, "cst", [128, 4], F32)
        ones32 = sb(es, "ones32", [128, 128], F32)
        y_naT = sb(es, "y_naT", [128, 4, SEQ], BF16)
        y_dilT = sb(es, "y_dilT", [128, 2, SEQ], BF16)
        S.dma("sp", identb[:], identb_d[:, :], "c0", writes=["identb"])
        S.dma("sp", identf[:], identf_d[:, :], "c1", writes=["identf"])
        S.dma("sp", gcols[:], gcols_d[:, :], "c2", writes=["gcols"])
        S.dma("sp", cst[:], cst_d[:, :], "c3", writes=["cst"])
        S.op("pool", lambda e: e.memset(ones32[:], 1.0), writes=["ones32"])
        eps_col = cst[:, 2:3]

        def debug_dump(name, ap, shape, dt, reads):
            if name not in debug:
                return
            t = nc.dram_tensor("dbg_" + name, list(shape), dt, kind="ExternalOutput").ap()
            dbg_out[name] = t
            S.dma("sp", t, ap, "dbg_" + name, reads=reads)

        def norm_transpose(stack_bufs, t_glob, dst, dst_keys_fn, gi, xres=None):
            xb, junk, ss, an = stack_bufs
            par = t_glob % 2
            xp = t_glob % len(xb)
            xt = xb[xp]
            kx = ("xt", xp)
            S.dma("sp", xt[:], x_d[t_glob * 128:(t_glob + 1) * 128, :], ("x", xp), writes=[kx])
            S.op("act", lambda e: e.activation(out=junk[:], in_=xt[:], func=AF.Square,
                                               accum_out=ss[:, par:par + 1]),
                 reads=[kx], writes=["junk", ("ss", par)])
            S.op("act", lambda e: e.activation(out=ss[:, par:par + 1], in_=ss[:, par:par + 1], func=AF.Ln,
                                               scale=1.0 / D, bias=eps_col),
                 reads=[("ss", par), "cst"], writes=[("ss", par)])
            S.op("act", lambda e: e.activation(out=ss[:, par:par + 1], in_=ss[:, par:par + 1], func=AF.Exp,
                                               scale=-0.5),
                 reads=[("ss", par)], writes=[("ss", par)])
            if (t_glob // 2) % 2 == 0:
                S.op("act", lambda e: e.activation(out=an[par][:], in_=xt[:], func=AF.Copy, scale=ss[:, par:par + 1]),
                     reads=[kx, ("ss", par)], writes=[("an", par)])
            else:
                S.op("dve", lambda e: e.tensor_scalar(out=an[par][:], in0=xt[:], scalar1=ss[:, par:par + 1],
                                                      scalar2=None, op0=ALU.mult),
                     reads=[kx, ("ss", par)], writes=[("an", par)])
            return xt, kx

        mix = ExitStack()
        es.enter_context(mix)
        aT = sb(mix, "aT", [128, 8, SEQ], BF16)
        aT_keys = [("aT", t) for t in range(NT)]
        qT = sb(mix, "qT", [128, SEQ], BF16)
        kTz = [sb(mix, f"kTz{i}", [128, SEQ], BF16) for i in range(2)]
        vv = sb(mix, "vv", [128, NT, 2, 65], BF16)
        wmx = sb(mix, "wmx", [128, 5120], BF16)
        PT = [sb(mix, f"PT{i}", [128, 512], BF16) for i in range(6)]
        rrow = sb(mix, "rrow", [128, 512], F32)
        S.op("pool", lambda e: e.memset(vv[:], 1.0), writes=["vv"])
        S.op("pool", lambda e: e.memset(rrow[:], 1.0), writes=["rrow"])
        ptr_b = banks[7][:, :].bitcast(BF16)

        def normalise(src65, src_keys, n, dst_ap, dst_key):
            S.op("act", lambda e: e.activation(out=rrow[64:65, 0:n], in_=src65[64:65, :], func=AF.Ln),
                 reads=src_keys, writes=["rrow"])
            S.op("act", lambda e: e.activation(out=rrow[64:65, 0:n], in_=rrow[64:65, 0:n], func=AF.Exp, scale=-1.0),
                 reads=["rrow"], writes=["rrow"])
            S.op("pe", lambda e: e.matmul(banks[7][0:64, 0:n], lhsT=ones32[64:65, 0:64], rhs=rrow[64:65, 0:n],
                                          start=True, stop=True),
                 reads=["rrow", "ones32"], writes=[("ps", 7)])
            S.op("dve", lambda e: e.tensor_tensor(out=dst_ap, in0=src65[0:64, :], in1=banks[7][0:64, 0:n], op=ALU.mult),
                 reads=list(src_keys) + [("ps", 7)], writes=[dst_key])

        with ExitStack() as phd:
            Ctab = sb(phd, "Ctab", [128, SEQ], F32)
            Stab = sb(phd, "Stab", [128, SEQ], F32)
            dmask = sb(phd, "dmask", [128, 3, 128], F32)
            ss = sb(phd, "ss", [128, 2], F32)
            S.dma("sp", dmask[:], dmask_d.rearrange("p (r f) -> p r f", r=3), "c4", writes=["dmask"])
            k0f = kTz[0][:, :].bitcast(F32)
            q0f = qT[:, :].bitcast(F32)
            xb = [k0f[:, 0:1024], k0f[:, 1024:2048], q0f[:, 0:1024], q0f[:, 1024:2048]]
            an = [kTz[1][:, 0:1024], kTz[1][:, 1024:2048]]
            junk = kTz[1][:, 2048:3072]
            wf = wmx[:, :].bitcast(F32)
            wi = wmx[:, :].bitcast(I32)
            ang, kf, rc = wf[:, 0:512], wf[:, 512:1024], wf[:, 1024:1536]
            posi, ki = wi[:, 1536:2048], wi[:, 2048:2560]

            def table_chunk(ch):
                cs_ = slice(ch * 512, (ch + 1) * 512)
                S.dma("pool", posi, pos_d[0:1, cs_].partition_broadcast(128), "posi", writes=["posi"])
                S.op("dve", lambda e: e.tensor_copy(out=ang, in_=posi), reads=["posi"], writes=["ang"])
                S.op("dve", lambda e: e.tensor_scalar(out=ang, in0=ang, scalar1=cst[:, 0:1], scalar2=None,
                                                       op0=ALU.mult), reads=["ang", "cst"], writes=["ang"])
                S.op("dve", lambda e: e.tensor_scalar(out=ki, in0=ang, scalar1=float(1.0 / TWO_PI),
                                                      scalar2=None, op0=ALU.mult), reads=["ang"], writes=["ki"])
                yield
                S.op("dve", lambda e: e.tensor_copy(out=kf, in_=ki), reads=["ki"], writes=["kf"])
                S.op("dve", lambda e: e.scalar_tensor_tensor(out=ang, in0=kf, scalar=-CW1, in1=ang,
                                                             op0=ALU.mult, op1=ALU.add),
                     reads=["kf", "ang"], writes=["ang"])
                S.op("dve", lambda e: e.scalar_tensor_tensor(out=ang, in0=kf, scalar=-CW2, in1=ang,
                                                             op0=ALU.mult, op1=ALU.add),
                     reads=["kf", "ang"], writes=["ang"])
                yield
                S.op("dve", lambda e: e.tensor_scalar(out=rc, in0=ang, scalar1=float(np.pi / 2), scalar2=None,
                                                       op0=ALU.add), reads=["ang"], writes=["rc"])
                S.op("dve", lambda e: e.tensor_scalar(out=kf, in0=rc, scalar1=float(np.pi),
                                                       scalar2=-TWO_PI, op0=ALU.is_gt, op1=ALU.mult),
                     reads=["rc"], writes=["kf"])
                S.op("dve", lambda e: e.tensor_tensor(out=rc, in0=rc, in1=kf, op=ALU.add),
                     reads=["rc", "kf"], writes=["rc"])
                yield
                S.op("dve", lambda e: e.tensor_scalar(out=Ctab[:, cs_], in0=rc, scalar1=PI_SAFE, scalar2=-PI_SAFE,
                                                       op0=ALU.min, op1=ALU.max), reads=["rc"], writes=[("Ctab", ch)])
                S.op("dve", lambda e: e.tensor_scalar(out=Stab[:, cs_], in0=ang, scalar1=PI_SAFE, scalar2=-PI_SAFE,
                                                       op0=ALU.min, op1=ALU.max), reads=["ang"], writes=[("Stab", ch)])

            def table_chunk_sin(ch):
                cs_ = slice(ch * 512, (ch + 1) * 512)
                S.op("act", lambda e: e.activation(out=Stab[:, cs_], in_=ang, func=AF.Sin, scale=cst[:, 1:2]),
                     reads=["ang", "cst"], writes=[("Stab", ch)])
                S.op("act", lambda e: e.activation(out=Ctab[:, cs_], in_=rc, func=AF.Sin),
                     reads=["rc"], writes=[("Ctab", ch)])

            gmix_b = gcols[:, 0:8].unsqueeze(2).broadcast_to([128, 8, 128])
            for t in range(NT):
                par = t % 2
                norm_transpose((xb, junk, ss, an), t, None, None, 0)
                pbk = 7 - (t % 2)
                ptr_t = banks[pbk][:, :].bitcast(BF16)
                for c in range(8):
                    S.op("pe", lambda e: e.transpose(out=ptr_t[:, c * 128:(c + 1) * 128],
                                                     in_=an[par][:, c * 128:(c + 1) * 128], identity=identb[:]),
                         reads=[("an", par), "identb"], writes=[("ps", pbk)], inc=(c == 7))
                S.op("dve", lambda e: e.tensor_tensor(
                    out=aT[:, :, t * 128:(t + 1) * 128],
                    in0=ptr_t.rearrange("p (c j) -> p c j", c=8), in1=gmix_b, op=ALU.mult),
                    reads=[("ps", pbk), "gcols"], writes=[("aT", t)])
                if t % 4 == 0:
                    tgen = table_chunk(t // 4)
                next(tgen, None)
            tkeys = [("Stab", ch) for ch in range(8)] + [("Ctab", ch) for ch in range(8)]
            S.op("act", lambda e: e.activation(out=Stab[:, :], in_=Stab[:, :], func=AF.Sin, scale=cst[:, 1:2]),
                 reads=tkeys + ["cst"], writes=tkeys)
            S.op("act", lambda e: e.activation(out=Ctab[:, :], in_=Ctab[:, :], func=AF.Sin), reads=tkeys, writes=tkeys)
            S.barrier()
            S.op("pool", lambda e: e.memset(kTz[0][:], 0.0), writes=["kT"])
            S.op("pool", lambda e: e.memset(kTz[1][:], 0.0), writes=["kT"])
            debug_dump("aT", aT[:], [128, 8, SEQ], BF16, [])
            debug_dump("Ctab", Ctab[:], [128, SEQ], F32, [])
            debug_dump("Stab", Stab[:], [128, SEQ], F32, [])
            accv = y_naT[:, :, :].rearrange("p a s -> p (a s)").bitcast(F32)
            acc = [accv[0:65, 0:SEQ], accv[0:65, SEQ:2 * SEQ]]
            t1 = [sb(phd, "t1_0", [128, 512], F32)] * 2
            t2 = [sb(phd, "t2_0", [128, 512], F32)] * 2

            def segs(d, n):
                L = SEQ // d
                if L >= 512:
                    r, m0 = (n * 512) // L, (n * 512) % L
                    return [(0, 512, r + d * m0, d)]
                out = []
                per = 512 // L
                for i in range(per):
                    out.append((i * L, L, n * per + i, d))
                return out

            def tsl(start, cnt, step):
                return slice(start, start + step * (cnt - 1) + 1, step)

            for P in range(2):
                for g in range(3):
                    d = DILS[g]
                    L = SEQ // d
                    bps = L // 128
                    pi = 2 * g + P
                    S.dma("pool", wmx[:, 0:5120], wD_d[pi], "wmx", writes=["wmx"])
                    wv = wmx[:, 0:5120].rearrange("p (k n) -> p k n", k=8)
                    for n in range(8):
                        nsl = slice(n * 512, (n + 1) * 512)
                        for which, dkey in ((0, "qT"), (1, "kT")):
                            bq, kq = bank([0, 1, 2, 3, 4, 5])
                            bs, ks = bank([0, 1, 2, 3, 4, 5])
                            for (bk, kk, off) in ((bq, kq, which * 256), (bs, ks, which * 256 + 128)):
                                for kc in range(8):
                                    S.op("pe", lambda e: e.matmul(
                                        bk[:, :], lhsT=wv[:, kc, off:off + 128], rhs=aT[:, kc, nsl],
                                        start=(kc == 0), stop=(kc == 7)),
                                        reads=["wmx"] + aT_keys, writes=[kk], inc=(kc == 7))
                            tp = 0
                            S.op("dve", lambda e: e.tensor_tensor(out=t1[tp][:], in0=bq[:, :], in1=Ctab[:, nsl], op=ALU.mult),
                                 reads=[kq], writes=[("t1", tp)])
                            S.op("act", lambda e: e.copy(out=t2[tp][:], in_=bs[:, :]), reads=[ks], writes=[("t2", tp)])
                            S.op("pool", lambda e: e.tensor_tensor(out=t2[tp][:], in0=t2[tp][:], in1=Stab[:, nsl], op=ALU.mult),
                                 reads=[("t2", tp)], writes=[("t2", tp)])
                            if which == 0:
                                parts = [(slice(0, 128), qT)]
                            else:
                                parts = [(slice(0, 64), kTz[0]), (slice(64, 128), kTz[1])]
                            for (psl, dstT) in parts:
                                if d == 1:
                                    S.op("dve", lambda e: e.tensor_tensor(out=dstT[psl, nsl], in0=t1[tp][psl, :],
                                                                          in1=t2[tp][psl, :], op=ALU.add),
                                         reads=[("t1", tp), ("t2", tp)], writes=[dkey])
                                else:
                                    w_ = 512 // d
                                    S.op("dve", lambda e: e.tensor_tensor(
                                        out=dstT[psl, :].rearrange("p (r l) -> p r l", r=d)[:, :, n * w_:(n + 1) * w_],
                                        in0=t1[tp][psl, :].rearrange("p (j r) -> p r j", r=d),
                                        in1=t2[tp][psl, :].rearrange("p (j r) -> p r j", r=d), op=ALU.add),
                                        reads=[("t1", tp), ("t2", tp)], writes=[dkey])
                    for nb0 in range(0, NT, 4):
                        bv, kv = bank([0, 1, 2, 3, 4, 5])
                        for i in range(4):
                            nb = nb0 + i
                            r, m0 = nb // bps, (nb % bps) * 128
                            sl = tsl(r + d * m0, 128, d)
                            for kc in range(8):
                                S.op("pe", lambda e: e.matmul(bv[:, i * 128:(i + 1) * 128], lhsT=aT[:, kc, sl],
                                                              rhs=wv[:, kc, 512:640], start=(kc == 0), stop=(kc == 7)),
                                     reads=["wmx"] + aT_keys, writes=[kv], inc=(i == 3 and kc == 7))
                        S.op("act", lambda e: e.copy(out=vv[:, nb0:nb0 + 4, :, 0:64],
                                                     in_=bv[:, :].rearrange("p (a h j) -> p a h j", a=4, h=2)),
                             reads=[kv], writes=[("vv", nb0 // 4)])
                    qkeys = ["qT"]
                    kkeys = ["kT"]
                    vkeys = [("vv", i) for i in range(8)] + ["vv"]
                    for hi in range(2):
                        hb = 64 * hi

                        def emit_qk(s4):
                            res = []
                            for rel in (-1, 0, 1):
                                bsc, ksc = bank([0, 1, 2, 3, 4, 5])
                                valid = []
                                for s in range(4):
                                    qb = 4 * s4 + s
                                    kb = qb + rel
                                    ok = 0 <= kb < NT and kb // bps == qb // bps
                                    valid.append(ok)
                                    if ok:
                                        S.op("pe", lambda e: e.matmul(
                                            bsc[:, s * 128:(s + 1) * 128], lhsT=kTz[hi][:, kb * 128:(kb + 1) * 128],
                                            rhs=qT[:, qb * 128:(qb + 1) * 128], start=True, stop=True),
                                            reads=qkeys + kkeys, writes=[ksc])
                                res.append((rel, bsc, ksc, valid))
                            return res

                        cur = emit_qk(0)
                        acc_pend = []
                        for s4 in range(8):
                            pts = []
                            for (rel, bsc, ksc, valid) in cur:
                                if not any(valid):
                                    pts.append(None)
                                    continue
                                pt_i = rot["pt"] % 6
                                rot["pt"] += 1
                                S.op("dve", lambda e: e.scalar_tensor_tensor(
                                    out=bsc[:, :].rearrange("p (a f) -> p a f", a=4),
                                    in0=bsc[:, :].rearrange("p (a f) -> p a f", a=4), scalar=0.125,
                                    in1=dmask[:, rel + 1, :].unsqueeze(1).broadcast_to([128, 4, 128]),
                                    op0=ALU.mult, op1=ALU.add), reads=[ksc, "dmask"], writes=[ksc])
                                S.op("act", lambda e: e.activation(out=PT[pt_i][:], in_=bsc[:, :], func=AF.Exp),
                                     reads=[ksc], writes=[("PT", pt_i)])
                                pts.append(pt_i)
                            nxt = emit_qk(s4 + 1) if s4 + 1 < 8 else None
                            if acc_pend:
                                acc_pend.pop(0)()
                            ob = 6 + (s4 % 2)
                            for s in range(4):
                                qb = 4 * s4 + s
                                use = [(ri, rel) for ri, (rel, _, _, valid) in enumerate(cur) if valid[s]]
                                for idx, (ri, rel) in enumerate(use):
                                    kb = qb + rel
                                    S.op("pe", lambda e: e.matmul(
                                        banks[ob][0:65, s * 128:(s + 1) * 128], lhsT=vv[:, kb, hi, 0:65],
                                        rhs=PT[pts[ri]][:, s * 128:(s + 1) * 128],
                                        start=(idx == 0), stop=(idx == len(use) - 1)),
                                        reads=vkeys + [("PT", pts[ri])], writes=[("ps", ob)],
                                        inc=(s == 3 and idx == len(use) - 1))

                            def acc_update(s4=s4, ob=ob):
                                for (c0, ncol, tstart, tstep) in segs(d, s4):
                                    sl = tsl(tstart, ncol, tstep)
                                    if g == 0:
                                        S.op("act", lambda e: e.copy(out=acc[hi][:, sl], in_=banks[ob][0:65, c0:c0 + ncol]),
                                             reads=[("ps", ob)], writes=[("acc", hi)])
                                    else:
                                        S.op("dve", lambda e: e.tensor_tensor(out=acc[hi][:, sl],
                                                                              in0=banks[ob][0:65, c0:c0 + ncol],
                                                                              in1=acc[hi][:, sl], op=ALU.add),
                                             reads=[("ps", ob), ("acc", hi)], writes=[("acc", hi)])
                            acc_pend.append(acc_update)
                            cur = nxt
                        while acc_pend:
                            acc_pend.pop(0)()
                for hi in range(2):
                    S.op("act", lambda e: e.activation(out=acc[hi][64:65, :], in_=acc[hi][64:65, :], func=AF.Ln),
                         reads=[("acc", hi)], writes=[("acc", hi)])
                    S.op("act", lambda e: e.activation(out=acc[hi][64:65, :], in_=acc[hi][64:65, :], func=AF.Exp,
                                                       scale=-1.0),
                         reads=[("acc", hi)], writes=[("acc", hi)])
                    for n in range(8):
                        sl = slice(n * 512, (n + 1) * 512)
                        bb, kb_ = bank([6, 7])
                        S.op("pe", lambda e: e.matmul(bb[0:64, :], lhsT=ones32[64:65, 0:64], rhs=acc[hi][64:65, sl],
                                                      start=True, stop=True),
                             reads=[("acc", hi), "ones32"], writes=[kb_])
                        S.op("dve", lambda e: e.tensor_tensor(out=y_dilT[64 * hi:64 * hi + 64, P, sl],
                                                              in0=acc[hi][0:64, sl], in1=bb[0:64, :], op=ALU.mult),
                             reads=[("acc", hi), kb_], writes=[("y_dilT", P, hi, n)])
            S.barrier()
        debug_dump("y_dilT", y_dilT[:], [128, 2, SEQ], BF16, [])
        if stop_after == "phd":
            S.finish(["dbg_" + k for k in dbg_out])
            return nc, dbg_out

        with ExitStack() as phn:
            U = [sb(phn, f"U{i}", [128, 2, 1408], F32) for i in range(2)]
            osb = sb(phn, "osb", [65, SEQ], F32)
            qtiles = [(0, 4, 1, list(range(0, 4))), (4, 4, 0, list(range(0, 6)))]
            for i in range(1, 7):
                qtiles.append((8 * i, 8, 0, list(range(4 * i - 2, 4 * i + 6))))
            qtiles += [(56, 4, 0, list(range(26, 32))), (60, 4, 1, list(range(28, 32)))]
            pending_bulk = []
            for pi in range(4):
                S.dma("pool", wmx[:, 0:3072], wN_d[pi], "wmx", writes=["wmx"])
                wv = wmx[:, 0:3072].rearrange("p (k n) -> p k n", k=8)
                for n in range(8):
                    for which in (0, 1):
                        bq, kq = bank([0, 1, 2, 3, 4, 5])
                        for kc in range(8):
                            S.op("pe", lambda e: e.matmul(bq[:, :], lhsT=wv[:, kc, which * 128:(which + 1) * 128],
                                                          rhs=aT[:, kc, n * 512:(n + 1) * 512],
                                                          start=(kc == 0), stop=(kc == 7)),
                                 reads=["wmx"] + aT_keys, writes=[kq], inc=(kc == 7))
                        nsl = slice(n * 512, (n + 1) * 512)
                        if which == 0:
                            S.op("act", lambda e: e.copy(out=qT[:, nsl], in_=bq[:, :]), reads=[kq], writes=[("qT", n)])
                        else:
                            S.op("dve", lambda e: e.tensor_copy(out=kTz[0][0:64, nsl], in_=bq[0:64, :]),
                                 reads=[kq], writes=[("kT", n, 0)])
                            S.op("pool" if False else "act", lambda e: e.copy(out=kTz[1][64:128, nsl], in_=bq[64:128, :]),
                                 reads=[kq], writes=[("kT", n, 1)])
                for nb0 in range(0, NT, 4):
                    bv, kv = bank([0, 1, 2, 3, 4, 5])
                    for i in range(4):
                        nb = nb0 + i
                        for kc in range(8):
                            S.op("pe", lambda e: e.matmul(bv[:, i * 128:(i + 1) * 128],
                                                          lhsT=aT[:, kc, nb * 128:(nb + 1) * 128],
                                                          rhs=wv[:, kc, 256:384], start=(kc == 0), stop=(kc == 7)),
                                 reads=["wmx"] + aT_keys, writes=[kv], inc=(i == 3 and kc == 7))
                    S.op("act", lambda e: e.copy(out=vv[:, nb0:nb0 + 4, :, 0:64],
                                                 in_=bv[:, :].rearrange("p (a h j) -> p a h j", a=4, h=2)),
                         reads=[kv], writes=[("vv", nb0 // 4)])
                qkeys = [("qT", n) for n in range(8)]
                kkeys = [("kT", n, i) for n in range(8) for i in range(2)] + ["kT"]
                vkeys = [("vv", i) for i in range(8)] + ["vv"]
                for hi in range(2):
                    h = 2 * pi + hi
                    hb = 64 * hi
                    Uh = U[h % 2]
                    S.dma("sp", Uh[:], rpbx_d[h].rearrange("t p f -> p t f"), ("U", h % 2), writes=[("U", h % 2)])
                    blocks = []
                    for qi, (R, nr, tsel, jl) in enumerate(qtiles):
                        rng = {}
                        for j in jl:
                            if tsel == 1:
                                rng[j] = (0, nr)
                            else:
                                dlt = 2 * j - R
                                lo, hi_ = max(0, dlt - 3), min(nr - 1, dlt + 5)
                                assert lo <= hi_
                                rng[j] = (lo, hi_ - lo + 1)
                        full = [j for j in jl if rng[j] == (0, nr)]
                        assert full
                        order = [full[0]] + [j for j in jl if j != full[0]]
                        for j in order:
                            blocks.append((qi, R, nr, tsel, j, j == order[0], j == order[-1], rng[j][0], rng[j][1]))

                    def emit_qk(bl):
                        qi, R, nr, tsel, j, first, last, blo, nb = bl
                        n = 64 * nb
                        q0 = 64 * (R + blo)
                        bsc, ksc = bank([0, 1, 2, 3, 6, 7])
                        S.op("pe", lambda e: e.matmul(bsc[:, 0:n], lhsT=kTz[hi][:, j * 128:(j + 1) * 128],
                                                      rhs=qT[:, q0:q0 + n], start=True, stop=True),
                             reads=qkeys + kkeys, writes=[ksc])
                        return bsc, ksc

                    LA = 4
                    pending = []
                    inflight = [emit_qk(b_) for b_ in blocks[:LA]]
                    while pending_bulk:
                        pending_bulk.pop(0)()
                    for bi, bl in enumerate(blocks):
                        qi, R, nr, tsel, j, first, last, blo, nb = bl
                        n = 64 * nb
                        bsc, ksc = inflight.pop(0)
                        m0 = 10 - (2 * j - R) + blo
                        pt_i = rot["pt"] % 6
                        rot["pt"] += 1
                        S.op("dve", lambda e: e.scalar_tensor_tensor(
                            out=bsc[:, 0:n], in0=bsc[:, 0:n], scalar=0.125,
                            in1=Uh[:, tsel, m0 * 64:(m0 + nb) * 64], op0=ALU.mult, op1=ALU.add),
                            reads=[ksc, ("U", h % 2)], writes=[ksc])
                        S.op("act", lambda e: e.activation(out=PT[pt_i][:, 0:n], in_=bsc[:, 0:n], func=AF.Exp),
                             reads=[ksc], writes=[("PT", pt_i)])
                        if bi + LA < len(blocks):
                            inflight.append(emit_qk(blocks[bi + LA]))
                        ob = 4 + (qi % 2)
                        S.op("pe", lambda e: e.matmul(banks[ob][0:65, 64 * blo:64 * blo + n], lhsT=vv[:, j, hi, 0:65],
                                                      rhs=PT[pt_i][:, 0:n], start=first, stop=last),
                             reads=vkeys + [("PT", pt_i)], writes=[("ps", ob)], inc=last)
                        if KEEP_WARM:
                            S.op("pe", lambda e: e.matmul(banks[7][:, :], lhsT=identb[:, :], rhs=aT[:, 0, 0:512],
                                                          start=True, stop=True),
                                 reads=[], writes=[], inc=False)
                        n = 64 * nr
                        if last:
                            def fin(qi=qi, n=n, ob=ob, R=R):
                                tsl_ = slice(64 * R, 64 * R + n)
                                if False:
                                    S.op("act", lambda e: e.copy(out=osb[:, tsl_], in_=banks[ob][0:65, 0:n]),
                                         reads=[("ps", ob)], writes=[("osb", qi)])
                                else:
                                    S.op("dve", lambda e: e.tensor_copy(out=osb[:, tsl_], in_=banks[ob][0:65, 0:n]),
                                         reads=[("ps", ob)], writes=[("osb", qi)])
                                S.op("act", lambda e: e.activation(out=osb[64:65, tsl_], in_=osb[64:65, tsl_], func=AF.Ln),
                                     reads=[("osb", qi)], writes=[("osb", qi)])
                                S.op("act", lambda e: e.activation(out=osb[64:65, tsl_], in_=osb[64:65, tsl_], func=AF.Exp,
                                                                   scale=-1.0),
                                     reads=[("osb", qi)], writes=[("osb", qi)])
                            pending.append((bi + 2, fin))
                        while pending and (pending[0][0] <= bi or bi == len(blocks) - 1):
                            pending.pop(0)[1]()
                    def bulk(hb=hb, pi=pi, hi=hi):
                        okeys = [("osb", qi) for qi in range(len(qtiles))]
                        for n8 in range(8):
                            nsl = slice(n8 * 512, (n8 + 1) * 512)
                            bb, kb_ = bank([4, 5])
                            S.op("pe", lambda e: e.matmul(bb[0:64, :], lhsT=ones32[64:65, 0:64], rhs=osb[64:65, nsl],
                                                          start=True, stop=True),
                                 reads=okeys + ["ones32"], writes=[kb_])
                            S.op("dve", lambda e: e.tensor_tensor(out=y_naT[hb:hb + 64, pi, nsl], in0=osb[0:64, nsl],
                                                                  in1=bb[0:64, :], op=ALU.mult),
                                 reads=okeys + [kb_], writes=[("y_naT", pi, hi, n8)])
                    pending_bulk.append(bulk)
            while pending_bulk:
                pending_bulk.pop(0)()
            S.barrier()
        debug_dump("y_naT", y_naT[:], [128, 4, SEQ], BF16, [])
        mix.close()
        if stop_after == "phn":
            S.finish(["dbg_" + k for k in dbg_out])
            return nc, dbg_out

        post = ExitStack()
        es.enter_context(post)
        wA = sb(post, "wA", [128, 4, 1024], BF16)
        wB = sb(post, "wB", [128, 2, 1024], BF16)
        wPP = sb(post, "wPP", [128, 2, 1024], BF16)
        S.dma("pool", wA[:], wA_d.rearrange("p (k n) -> p k n", k=4), "wA", writes=["wA"])
        S.dma("pool", wB[:], wB_d.rearrange("p (k n) -> p k n", k=2), "wB", writes=["wB"])
        S.dma("pool", wPP[:], wPP_d.rearrange("p (k n) -> p k n", k=2), "wPP", writes=["wPP"])
        NSLOT = 4
        slots = [sb(post, f"slot{i}", [128, 4096], BF16) for i in range(NSLOT)]
        xb = [sb(post, f"pxb{i}", [128, D], F32) for i in range(2)]
        junk = sb(post, "pjunk", [128, D], BF16)
        ss = sb(post, "pss", [128, 2], F32)
        an = [sb(post, f"pan{i}", [128, D], BF16) for i in range(2)]
        pb = [sb(post, f"ppb{i}", [128, 256], F32) for i in range(2)]
        pbb = [sb(post, f"ppbb{i}", [128, 256], BF16) for i in range(2)]
        aTg = sb(post, "aTg", [128, 8, 512], BF16)
        mxT = sb(post, "mxT", [128, 8, 512], BF16)
        pT = [sb(post, f"pT{i}", [128, 2, 512], BF16) for i in range(2)]
        hT = sb(post, "hT", [128, 8, 512], F32)
        uT = sb(post, "uT", [128, 32, 512], BF16)
        sg = [sb(post, f"sg{i}", [128, 512], F32) for i in range(2)]
        mm_ = [sb(post, f"mm{i}", [128, 512], F32) for i in range(2)]
        sq = [sb(post, f"sq{i}", [128, 512], BF16) for i in range(2)]
        onesb = sb(post, "onesb", [128, 128], BF16)
        S.op("pool", lambda e: e.memset(onesb[:], 1.0), writes=["onesb"])
        rbc = sb(post, "rbc", [128, 512], F32)
        ost = [sb(post, f"ost{i}", [128, D], F32) for i in range(2)]
        load_i = {"n": 0}
        POOLB = [0, 1, 2, 3, 4, 5, 6]

        def issue_load(l_glob):
            s = l_glob % NSLOT
            S.dma("pool", slots[s][:], wS_d[l_glob % N_LOADS], ("slot", s), writes=[("slot", s)])

        total_loads = 8 * N_LOADS
        for l in range(NSLOT):
            issue_load(l)
        load_i["n"] = NSLOT

        def next_slot(l_glob):
            return slots[l_glob % NSLOT], ("slot", l_glob % NSLOT)

        def done_slot():
            if load_i["n"] < total_loads:
                issue_load(load_i["n"])
                load_i["n"] += 1

        POOLB[:] = [0, 1, 2, 3, 4, 5]

        def sum_sq(c, first, last):
            S.op("act", lambda e: e.activation(out=sq[c % 2][:], in_=hT[:, c, :], func=AF.Square),
                 reads=[("hT", c)], writes=[("sq", c % 2)])
            S.op("pe", lambda e: e.matmul(banks[6][:, :], lhsT=onesb[:, :], rhs=sq[c % 2][:], start=first, stop=last),
                 reads=[("sq", c % 2), "onesb"], writes=[("ps", 6)])

        def rms_finish(gi, dstT, dkey):
            S.op("act", lambda e: e.activation(out=rbc[:], in_=banks[6][:, :], func=AF.Ln, scale=1.0 / D, bias=eps_col),
                 reads=[("ps", 6), "cst"], writes=["rbc"])
            S.op("act", lambda e: e.activation(out=rbc[:], in_=rbc[:], func=AF.Exp, scale=-0.5),
                 reads=["rbc"], writes=["rbc"])
            for c in range(8):
                S.op("dve", lambda e: e.scalar_tensor_tensor(out=dstT[:, c, :], in0=hT[:, c, :],
                                                              scalar=gcols[:, gi * 8 + c:gi * 8 + c + 1], in1=rbc[:],
                                                              op0=ALU.mult, op1=ALU.mult),
                     reads=[("hT", c), "rbc", "gcols"], writes=[(dkey, c)])

        def head_A_p1(grp, tt):
            t = grp * 4 + tt
            par = t % 2
            norm_transpose((xb, junk, ss, an), t, None, None, 0)
            S.dma("sp", pb[par][:], p_d[t * 128:(t + 1) * 128, :], ("pb", par), writes=[("pb", par)])
            S.op("act", lambda e: e.copy(out=pbb[par][:], in_=pb[par][:]), reads=[("pb", par)], writes=[("pbb", par)])

        def head_A_p2(grp, tt):
            t = grp * 4 + tt
            par = t % 2
            pTg = pT[grp % 2]
            for c in range(8):
                S.op("pe", lambda e: e.transpose(out=ptr_b[:, c * 128:(c + 1) * 128],
                                                 in_=an[par][:, c * 128:(c + 1) * 128], identity=identb[:]),
                     reads=[("an", par), "identb"], writes=[("ps", 7)], inc=(c == 7))
            S.op("dve", lambda e: e.tensor_tensor(
                out=aTg[:, :, tt * 128:(tt + 1) * 128], in0=ptr_b.rearrange("p (c j) -> p c j", c=8),
                in1=gcols[:, 0:8].unsqueeze(2).broadcast_to([128, 8, 128]), op=ALU.mult),
                reads=[("ps", 7), "gcols"], writes=[("aTg", c) for c in range(8)])
            for c2 in range(2):
                S.op("pe", lambda e: e.transpose(out=ptr_b[:, c2 * 128:(c2 + 1) * 128],
                                                 in_=pbb[par][:, c2 * 128:(c2 + 1) * 128], identity=identb[:]),
                     reads=[("pbb", par), "identb"], writes=[("ps", 7)], inc=(c2 == 1))
            S.op("act", lambda e: e.copy(out=pTg[:, :, tt * 128:(tt + 1) * 128],
                                         in_=ptr_b[:, 0:256].rearrange("p (c j) -> p c j", c=2)),
                 reads=[("ps", 7)], writes=[("pT", grp % 2)])

        def head_A_tile(grp, tt):
            head_A_p1(grp, tt)
            head_A_p2(grp, tt)

        def head_A(grp):
            for tt in range(4):
                head_A_tile(grp, tt)

        def head_B_p1(grp, tt):
            t = grp * 4 + tt
            par = t % 2
            S.dma("sp", ost[par][:], x_d[t * 128:(t + 1) * 128, :], ("x2", par),
                  writes=[("ost", par, 0), ("ost", par, 1)])

        def head_B_p2(grp, tt):
            t = grp * 4 + tt
            par = t % 2
            for half in range(2):
                bx, kxb = bank(POOLB)
                for c4 in range(4):
                    c = half * 4 + c4
                    S.op("pe", lambda e: e.transpose(out=bx[:, c4 * 128:(c4 + 1) * 128],
                                                     in_=ost[par][:, c * 128:(c + 1) * 128], identity=identf[:]),
                         reads=[("ost", par, half), "identf"], writes=[kxb], inc=(c4 == 3))
                if half == 0:
                    S.op("act", lambda e: e.copy(out=hT[:, 0:4, tt * 128:(tt + 1) * 128],
                                                 in_=bx[:, :].rearrange("p (c j) -> p c j", c=4)),
                         reads=[kxb], writes=[("hT", c4) for c4 in range(4)])
                else:
                    S.op("dve", lambda e: e.tensor_copy(out=hT[:, 4:8, tt * 128:(tt + 1) * 128],
                                                        in_=bx[:, :].rearrange("p (c j) -> p c j", c=4)),
                         reads=[kxb], writes=[("hT", 4 + c4) for c4 in range(4)])

        lgc = {"n": 0}

        def gates_half(grp, half, with_head_b):
            tok0 = grp * 512
            lg = lgc["n"]
            sl_na, k_na = next_slot(lg)
            sl_dl, k_dl = next_slot(lg + 1)
            wna = sl_na[:, :].rearrange("p (o k j) -> p o k j", o=4, k=8)
            wdl = sl_dl[:, :].rearrange("p (o k j) -> p o k j", o=4, k=8)
            for o in range(4):
                c = half * 4 + o
                b1, k1 = bank(POOLB)
                for kc in range(8):
                    S.op("pe", lambda e: e.matmul(b1[:, :], lhsT=wna[:, o, kc, :], rhs=aTg[:, kc, :],
                                                  start=(kc == 0), stop=(kc == 7)),
                         reads=[k_na, ("aTg", kc)], writes=[k1], inc=(kc == 7))
                S.op("act", lambda e: e.activation(out=sg[0][:], in_=b1[:, :], func=AF.Sigmoid),
                     reads=[k1], writes=[("sg", 0)])
                b2, k2 = bank(POOLB)
                for kc in range(8):
                    S.op("pe", lambda e: e.matmul(b2[:, :], lhsT=wdl[:, o, kc, :], rhs=aTg[:, kc, :],
                                                  start=(kc == 0), stop=(kc == 7)),
                         reads=[k_dl, ("aTg", kc)], writes=[k2], inc=(kc == 7))
                S.op("act", lambda e: e.activation(out=sg[1][:], in_=b2[:, :], func=AF.Sigmoid),
                     reads=[k2], writes=[("sg", 1)])
                b3, k3 = bank(POOLB)
                for kc in range(4):
                    S.op("pe", lambda e: e.matmul(b3[:, :], lhsT=wA[:, kc, c * 128:(c + 1) * 128],
                                                  rhs=y_naT[:, kc, tok0:tok0 + 512], start=(kc == 0), stop=(kc == 3)),
                         reads=["wA"], writes=[k3], inc=(kc == 3))
                S.op("dve", lambda e: e.tensor_tensor(out=mm_[0][:], in0=b3[:, :], in1=sg[0][:], op=ALU.mult),
                     reads=[k3, ("sg", 0)], writes=[("mm", 0)])
                b4, k4 = bank(POOLB)
                for kc in range(2):
                    S.op("pe", lambda e: e.matmul(b4[:, :], lhsT=wB[:, kc, c * 128:(c + 1) * 128],
                                                  rhs=y_dilT[:, kc, tok0:tok0 + 512], start=(kc == 0), stop=(kc == 1)),
                         reads=["wB"], writes=[k4], inc=(kc == 1))
                S.op("dve", lambda e: e.tensor_tensor(out=mm_[1][:], in0=b4[:, :], in1=sg[1][:], op=ALU.mult),
                     reads=[k4, ("sg", 1)], writes=[("mm", 1)])
                S.op("pool", lambda e: e.tensor_tensor(out=mxT[:, c, :], in0=mm_[0][:], in1=mm_[1][:], op=ALU.add),
                     reads=[("mm", 0), ("mm", 1)], writes=[("mxT", c)])
                if half == 0 and o == 1 and grp > 0:
                    final_norm()
                if with_head_b:
                    if o == 1:
                        if grp > 0:
                            output_stage(grp - 1)
                        head_B_p1(grp, 0)
                        head_B_p1(grp, 1)
                    elif o == 2:
                        head_B_p2(grp, 0)
                        head_B_p1(grp, 2)
                    elif o == 3:
                        head_B_p2(grp, 1)
                        head_B_p1(grp, 3)
            lgc["n"] += 2
            done_slot()
            done_slot()

        def output_stage(grp):
            for tt in range(4):
                t = grp * 4 + tt
                par = t % 2
                for half in range(2):
                    bx, kxb = bank(POOLB)
                    for c4 in range(4):
                        c = half * 4 + c4
                        S.op("pe", lambda e: e.transpose(out=bx[:, c4 * 128:(c4 + 1) * 128],
                                                         in_=hT[:, c, tt * 128:(tt + 1) * 128], identity=identf[:]),
                             reads=[("hT", c), "identf"], writes=[kxb], inc=(c4 == 3))
                    if half == 0:
                        S.op("act", lambda e: e.copy(out=ost[par][:, 0:512], in_=bx[:, :]),
                             reads=[kxb], writes=[("ost", par, 0)])
                    else:
                        S.op("dve", lambda e: e.tensor_copy(out=ost[par][:, 512:1024], in_=bx[:, :]),
                             reads=[kxb], writes=[("ost", par, 1)])
                S.dma("sp", out_d[t * 128:(t + 1) * 128, :], ost[par][:], ("out", par),
                      reads=[("ost", par, 0), ("ost", par, 1)])

        def final_norm():
            for c in range(8):
                sum_sq(c, c == 0, c == 7)
            rms_finish(3, hT, "hT")

        head_A(0)
        for grp in range(8):
            pTg = pT[grp % 2]
            pkey = ("pT", grp % 2)
            gates_half(grp, 0, False)
            gates_half(grp, 1, True)
            head_B_p2(grp, 2)
            head_B_p2(grp, 3)
            for half in range(2):
                sl, ksl = next_slot(lgc["n"])
                wv_ = sl[:, :].rearrange("p (o k j) -> p o k j", o=4, k=8)
                for o in range(4):
                    c = half * 4 + o
                    b1, k1 = bank(POOLB)
                    for kc in range(8):
                        S.op("pe", lambda e: e.matmul(b1[:, :], lhsT=wv_[:, o, kc, :], rhs=mxT[:, kc, :],
                                                      start=(kc == 0), stop=(kc == 7)),
                             reads=[ksl, ("mxT", kc)], writes=[k1], inc=(kc == 7))
                    if c > 1:
                        sum_sq(c - 2, c == 2, False)
                    S.op("dve", lambda e: e.tensor_tensor(out=hT[:, c, :], in0=b1[:, :], in1=hT[:, c, :], op=ALU.add),
                         reads=[k1, ("hT", c)], writes=[("hT", c)])
                lgc["n"] += 1
                done_slot()
            sum_sq(6, False, False)
            sum_sq(7, False, True)
            rms_finish(1, mxT, "mxT")
            for i in range(8):
                sl, ksl = next_slot(lgc["n"])
                wv_ = sl[:, :].rearrange("p (o k j) -> p o k j", o=4, k=8)
                bks = [bank(POOLB) for _ in range(4)]
                if i == 0:
                    for kc in range(8):
                        for o in range(4):
                            S.op("pe", lambda e: e.matmul(bks[o][0][:, :], lhsT=wv_[:, o, kc, :], rhs=mxT[:, kc, :],
                                                          start=(kc == 0), stop=(kc == 7)),
                                 reads=[ksl, ("mxT", kc)], writes=[bks[o][1]], inc=(kc == 7))
                else:
                    for o in range(4):
                        for kc in range(8):
                            S.op("pe", lambda e: e.matmul(bks[o][0][:, :], lhsT=wv_[:, o, kc, :], rhs=mxT[:, kc, :],
                                                          start=(kc == 0), stop=(kc == 7)),
                                 reads=[ksl, ("mxT", kc)], writes=[bks[o][1]], inc=(kc == 7))
                for o in range(4):
                    fc = 4 * i + o
                    ri = fc % 2
                    S.op("act", lambda e: e.activation(out=sg[ri][:], in_=bks[o][0][:, :], func=AF.Relu),
                         reads=[bks[o][1]], writes=[("sg", ri)])
                    S.op("pool", lambda e: e.tensor_tensor(out=uT[:, fc, :], in0=sg[ri][:], in1=sg[ri][:], op=ALU.mult),
                         reads=[("sg", ri)], writes=[("uT", fc)])
                lgc["n"] += 1
                done_slot()
                if grp + 1 < 8 and i % 2 == 0:
                    if i >= 2:
                        head_A_p2(grp + 1, i // 2 - 1)
                    head_A_p1(grp + 1, i // 2)
            if grp + 1 < 8:
                head_A_p2(grp + 1, 3)
            for c in range(8):
                sl, ksl = next_slot(lgc["n"])
                wv_ = sl[:, :].rearrange("p (k j) -> p k j", k=32)
                b1, k1 = bank(POOLB)
                for kc in range(32):
                    S.op("pe", lambda e: e.matmul(b1[:, :], lhsT=wv_[:, kc, :], rhs=uT[:, kc, :],
                                                  start=(kc == 0), stop=(kc == 31)),
                         reads=[ksl, ("uT", kc)], writes=[k1], inc=(kc == 31))
                if c > 0:
                    sum_sq(c - 1, c == 1, False)
                S.op("dve", lambda e: e.tensor_tensor(out=hT[:, c, :], in0=b1[:, :], in1=hT[:, c, :], op=ALU.add),
                     reads=[k1, ("hT", c)], writes=[("hT", c)])
                lgc["n"] += 1
                done_slot()
            sum_sq(7, False, True)
            rms_finish(2, mxT, "mxT")
            for half in range(2):
                sl, ksl = next_slot(lgc["n"])
                wv_ = sl[:, :].rearrange("p (o k j) -> p o k j", o=4, k=8)
                bks = [bank(POOLB) for _ in range(4)]
                if half == 0:
                    for kc in range(8):
                        for o in range(4):
                            S.op("pe", lambda e: e.matmul(bks[o][0][:, :], lhsT=wv_[:, o, kc, :], rhs=mxT[:, kc, :],
                                                          start=(kc == 0), stop=(kc == 7)),
                                 reads=[ksl, ("mxT", kc)], writes=[bks[o][1]], inc=(kc == 7))
                else:
                    for o in range(4):
                        for kc in range(8):
                            S.op("pe", lambda e: e.matmul(bks[o][0][:, :], lhsT=wv_[:, o, kc, :], rhs=mxT[:, kc, :],
                                                          start=(kc == 0), stop=(kc == 7)),
                                 reads=[ksl, ("mxT", kc)], writes=[bks[o][1]], inc=(kc == 7))
                for o in range(4):
                    c = half * 4 + o
                    S.op("act", lambda e: e.activation(out=sg[o % 2][:], in_=bks[o][0][:, :], func=AF.Sigmoid),
                         reads=[bks[o][1]], writes=[("sg", o % 2)])
                    b2, k2 = bank(POOLB)
                    for kc in range(2):
                        S.op("pe", lambda e: e.matmul(b2[:, :], lhsT=wPP[:, kc, c * 128:(c + 1) * 128], rhs=pTg[:, kc, :],
                                                      start=(kc == 0), stop=(kc == 1)),
                             reads=["wPP", pkey], writes=[k2], inc=(kc == 1))
                    S.op("dve", lambda e: e.tensor_tensor(out=mm_[o % 2][:], in0=b2[:, :], in1=sg[o % 2][:], op=ALU.mult),
                         reads=[k2, ("sg", o % 2)], writes=[("mm", o % 2)])
                    S.op("pool", lambda e: e.tensor_tensor(out=hT[:, c, :], in0=hT[:, c, :], in1=mm_[o % 2][:], op=ALU.add),
                         reads=[("mm", o % 2), ("hT", c)], writes=[("hT", c)])
                lgc["n"] += 1
                done_slot()
        final_norm()
        output_stage(7)
        S.finish([("out", 0), ("out", 1)] + ["dbg_" + k for k in dbg_out])
        post.close()
        print("inst", S.n_inst, "waits", S.n_wait, "nsem", S.nsem, flush=True)
    return nc, dbg_out


def kernel(**inputs):
    sh = _prep_shared(inputs)
    x = np.asarray(inputs["x"], np.float32)
    p = np.asarray(inputs["p"], np.float32)[0]
    pos = np.asarray(inputs["positions"], np.int32)
    nc = build_program()[0]
    in_maps = []
    for b in range(8):
        m = dict(sh)
        m["x"] = np.ascontiguousarray(x[b])
        m["p"] = np.ascontiguousarray(p[b])
        m["pos"] = np.ascontiguousarray(pos[b:b + 1])
        in_maps.append(m)
    res = run_bass_kernel_spmd(nc, in_maps, core_ids=list(range(8)))
    return np.stack([np.asarray(r["out"], np.float32) for r in res.results], axis=0)
```

```python
import numpy as np
import ml_dtypes
from contextlib import ExitStack
import concourse.bass as bass
import concourse.mybir as mybir
from concourse.bass_utils import run_bass_kernel_spmd

F32 = mybir.dt.float32
BF16 = mybir.dt.bfloat16
I32 = mybir.dt.int32
ALU = mybir.AluOpType
AF = mybir.ActivationFunctionType

SEQ = 4096
D = 1024
NT = SEQ // 128
NEG = -30000.0
PI_SAFE = 3.1415925
TWO_PI = 6.283185307179586
CW1 = 6.28125
CW2 = TWO_PI - CW1
DILS = (1, 4, 16)
N_LOADS = 24
KEEP_WARM = False


class Sched:
    EPOCH = 30000

    def __init__(self, nc, es):
        self.nc = nc
        self.es = es
        self.eng = {"pe": nc.tensor, "act": nc.scalar, "dve": nc.vector,
                    "pool": nc.gpsimd, "sp": nc.sync}
        self.csem = {}
        self.cnt = {}
        self.nsem = 0
        for e in ("pe", "act", "dve", "pool"):
            self._new_epoch(e)
        self.dsem = {}
        self.dsem_by_id = {}
        self.seen = {e: {} for e in self.eng}
        self.lastw = {}
        self.readers = {}
        self.pending = {e: False for e in self.eng}
        self.n_inst = {e: 0 for e in self.eng}
        self.n_wait = {e: 0 for e in self.eng}
        self.all_sems = []

    def _new_sem(self, name):
        self.nsem += 1
        s = self.es.enter_context(self.nc.semaphore(f"{name}_{self.nsem}"))
        return s

    def _new_epoch(self, e):
        self.csem[e] = self._new_sem("c" + e)
        self.cnt[e] = 0

    def _deps(self, reads, writes):
        toks = []
        for r in reads:
            t = self.lastw.get(r)
            if t is not None:
                toks.append(t)
        for w in writes:
            t = self.lastw.get(w)
            if t is not None:
                toks.append(t)
            toks.extend(self.readers.get(w, {}).values())
        return toks

    def _emit_waits(self, e, toks):
        need = {}
        for (sem, val, src) in toks:
            if src == "pe" and e == "pe":
                continue
            k = id(sem)
            if k in self.dsem_by_id:
                val = max(val, self.dsem_by_id[k][1])
            if src in self.csem and sem is self.csem[src]:
                assert val <= self.cnt[src], f"wait on un-incremented op of {src}"
            if self.seen[e].get(k, 0) >= val:
                continue
            if k not in need or need[k][1] < val:
                need[k] = (sem, val)
        for k, (sem, val) in need.items():
            self.eng[e].wait_ge(sem, val)
            self.seen[e][k] = val
            self.n_wait[e] += 1

    def _record(self, tok, who, reads, writes):
        for w in writes:
            self.lastw[w] = tok
            self.readers[w] = {}
        for r in reads:
            if r in writes:
                continue
            self.readers.setdefault(r, {})[who] = tok

    def op(self, e, fn, reads=(), writes=(), inc=True):
        reads = tuple(reads)
        writes = tuple(writes)
        self._emit_waits(e, self._deps(reads, writes))
        inst = fn(self.eng[e])
        self.n_inst[e] += 1
        if self.cnt[e] >= self.EPOCH and not self.pending[e]:
            self._new_epoch(e)
        if inc:
            inst.then_inc(self.csem[e], 1)
            self.cnt[e] += 1
            tok = (self.csem[e], self.cnt[e], e)
            self.pending[e] = False
        else:
            tok = (self.csem[e], self.cnt[e] + 1, e)
            self.pending[e] = True
        self._record(tok, e, reads, writes)
        return inst

    def dma(self, q, out, in_, semkey, reads=(), writes=()):
        reads = tuple(reads)
        writes = tuple(writes)
        self._emit_waits(q, self._deps(reads, writes))
        if semkey not in self.dsem:
            self.dsem[semkey] = [self._new_sem("d"), 0]
            self.dsem_by_id[id(self.dsem[semkey][0])] = self.dsem[semkey]
        ent = self.dsem[semkey]
        inst = self.eng[q].dma_start(out=out, in_=in_)
        inst.then_inc(ent[0], 16)
        ent[1] += 16
        self.n_inst[q] += 1
        who = "dma:" + str(semkey)
        tok = (ent[0], ent[1], who)
        self._record(tok, who, reads, writes)
        return inst

    def barrier(self):
        toks = []
        for e in ("pe", "act", "dve", "pool"):
            assert not self.pending[e]
            if self.cnt[e] > 0:
                toks.append((self.csem[e], self.cnt[e], e))
        for k, ent in self.dsem.items():
            if ent[1] > 0:
                toks.append((ent[0], ent[1], "dma:" + str(k)))
        for e in self.eng:
            self._emit_waits(e, [t for t in toks if not (t[2] == e and e != "pe")] +
                             [t for t in toks if t[2] == e and e != "pe"])
        self.lastw = {}
        self.readers = {}

    def finish(self, keys):
        for k in keys:
            ent = self.dsem[k]
            self.nc.sync.wait_ge(ent[0], ent[1])
        for e, p in self.pending.items():
            assert not p


def _blk(w):
    K, N = w.shape
    return np.ascontiguousarray(w.reshape(K // 128, 128, N // 128, 128).transpose(2, 1, 0, 3))


def _prep_shared(inp):
    f32 = np.float32
    w_in = np.asarray(inp["w_in"][0], f32)
    sh = {}
    swap = np.concatenate([np.arange(32, 64), np.arange(0, 32)])
    wD = np.empty((6, 128, 8, 640), f32)
    for g in range(3):
        for P in range(2):
            h0 = 4 * g + 2 * P
            cq = 1536 + 64 * h0 + np.arange(128)
            ck = 2304 + 64 * h0 + np.arange(128)
            cv = 3072 + 64 * h0 + np.arange(128)
            sw = np.concatenate([swap, 64 + swap])
            cols = np.concatenate([cq, cq[sw], ck, ck[sw], cv])
            blk = w_in[:, cols].reshape(8, 128, 640).transpose(1, 0, 2)
            wD[2 * g + P] = blk
    sh["wD"] = wD.reshape(6, 128, 8 * 640)
    wN = np.empty((4, 128, 8, 384), f32)
    for pi in range(4):
        cq = 128 * pi + np.arange(128)
        cols = np.concatenate([cq, 512 + cq, 1024 + cq])
        wN[pi] = w_in[:, cols].reshape(8, 128, 384).transpose(1, 0, 2)
    sh["wN"] = wN.reshape(4, 128, 8 * 384)
    gna = _blk(w_in[:, 3840:4864])
    gdil = _blk(w_in[:, 4864:5888])
    wo = _blk(np.asarray(inp["w_out"][0], f32))
    wu = _blk(np.asarray(inp["w_up"][0], f32))
    wd = _blk(np.asarray(inp["w_down"][0], f32))
    wpg = _blk(np.asarray(inp["w_ple_gate"][0], f32))

    def four(b, i):
        return b[4 * i:4 * i + 4].transpose(1, 0, 2, 3).reshape(128, 4096)

    loads = [four(gna, 0), four(gdil, 0), four(gna, 1), four(gdil, 1), four(wo, 0), four(wo, 1)]
    loads += [four(wu, i) for i in range(8)]
    loads += [wd[c].reshape(128, 4096) for c in range(8)]
    loads += [four(wpg, 0), four(wpg, 1)]
    assert len(loads) == N_LOADS
    sh["wS"] = np.ascontiguousarray(np.stack(loads))
    def pk(w):
        K, N = w.shape
        return np.ascontiguousarray(w.reshape(K // 128, 128, N).transpose(1, 0, 2).reshape(128, -1))
    sh["wA"] = pk(np.asarray(inp["w_branch_na"][0], f32))
    sh["wB"] = pk(np.asarray(inp["w_branch_dil"][0], f32))
    sh["wPP"] = pk(np.asarray(inp["w_ple_proj"][0], f32))
    gs = [inp["g_mix"][0], inp["g_mlp"][0], inp["g_ple"][0], inp["g_final"]]
    sh["gcols"] = np.ascontiguousarray(
        np.stack([np.asarray(g, f32).reshape(8, 128).T for g in gs], axis=1).reshape(128, 32))
    rpb = np.asarray(inp["rpb"][0], f32)
    kc = np.arange(64)[:, None]
    qc = np.arange(64)[None, :]
    cs = np.clip(qc - 8, 0, 48)
    colvalid = (kc >= cs) & (kc < cs + 16)
    coff = np.clip(kc - qc, -15, 15) + 15
    tab = np.empty((8, 2, 2, 64, 22, 64), f32)
    for a in range(2):
        for m in range(22):
            dr = 10 + a - m
            inr = abs(dr) <= 7
            vals = rpb[:, min(max(dr, -7), 7) + 7][:, coff]
            allv = np.where(colvalid[None] & inr, vals, f32(NEG))
            midv = np.where(colvalid[None] & (dr >= -4) & (dr <= 3), vals, f32(NEG))
            tab[:, 0, a, :, m, :] = midv
            tab[:, 1, a, :, m, :] = allv
    sh["rpbx"] = np.ascontiguousarray(tab.reshape(8, 2, 128, 22 * 64))
    sh["ident_bf"] = np.eye(128, dtype=f32).astype(ml_dtypes.bfloat16)
    sh["ident_f"] = np.eye(128, dtype=f32)
    half = 32
    invf = (f32(10000.0) ** (-(np.arange(half, dtype=f32)) / f32(half))).astype(f32)
    cst = np.zeros((128, 4), f32)
    cst[:, 0] = invf[np.arange(128) % 32]
    cst[:, 1] = np.where((np.arange(128) % 64) < 32, -1.0, 1.0)
    cst[:, 2] = 1e-6
    cst[:, 3] = 1.0
    sh["cst"] = cst
    p = np.arange(128)[:, None]
    f = np.arange(128)[None, :]
    dm = np.stack([np.where(p >= f + 64, 0.0, NEG), np.where(np.abs(p - f) <= 64, 0.0, NEG),
                   np.where(p <= f - 64, 0.0, NEG)], axis=1).astype(f32)
    sh["dmask"] = np.ascontiguousarray(dm.reshape(128, 384))
    return sh


def build_program(debug=(), stop_after=None):
    nc = bass.Bass("TRN2", target_bir_lowering=False)

    def din(name, shape, dt):
        return nc.dram_tensor(name, list(shape), dt, kind="ExternalInput").ap()

    x_d = din("x", [SEQ, D], F32)
    p_d = din("p", [SEQ, 256], F32)
    pos_d = din("pos", [1, SEQ], I32)
    gcols_d = din("gcols", [128, 32], F32)
    cst_d = din("cst", [128, 4], F32)
    identb_d = din("ident_bf", [128, 128], BF16)
    identf_d = din("ident_f", [128, 128], F32)
    dmask_d = din("dmask", [128, 384], F32)
    wD_d = din("wD", [6, 128, 5120], F32)
    wN_d = din("wN", [4, 128, 3072], F32)
    wS_d = din("wS", [N_LOADS, 128, 4096], F32)
    wA_d = din("wA", [128, 4096], F32)
    wB_d = din("wB", [128, 2048], F32)
    wPP_d = din("wPP", [128, 2048], F32)
    rpbx_d = din("rpbx", [8, 2, 128, 1408], F32)
    out_d = nc.dram_tensor("out", [SEQ, D], F32, kind="ExternalOutput").ap()
    dbg_out = {}

    with ExitStack() as es:
        S = Sched(nc, es)

        def sb(stack, name, shape, dt):
            return stack.enter_context(nc.sbuf_tensor("s_" + name, list(shape), dt))

        banks = [es.enter_context(nc.psum_tensor(f"bank{i}", [128, 512], F32)) for i in range(8)]
        rot = {"pt": 0}

        def bank(pool):
            k = tuple(pool)
            i = rot.get(k, 0)
            rot[k] = i + 1
            b = pool[i % len(pool)]
            return banks[b], ("ps", b)

        identb = sb(es, "identb", [128, 128], BF16)
        identf = sb(es, "identf", [128, 128], F32)
        gcols = sb(es, "gcols", [128, 32], F32)
        cst = sb(es, "cst", [128, 4], F32)
        ones32 = sb(es, "ones32", [128, 128], F32)
        y_naT = sb(es, "y_naT", [128, 4, SEQ], BF16)
        y_dilT = sb(es, "y_dilT", [128, 2, SEQ], BF16)
        S.dma("sp", identb[:], identb_d[:, :], "c0", writes=["identb"])
        S.dma("sp", identf[:], identf_d[:, :], "c1", writes=["identf"])
        S.dma("sp", gcols[:], gcols_d[:, :], "c2", writes=["gcols"])
        S.dma("sp", cst[:], cst_d[:, :], "c3", writes=["cst"])
        S.op("pool", lambda e: e.memset(ones32[:], 1.0), writes=["ones32"])
        eps_col = cst[:, 2:3]

        def debug_dump(name, ap, shape, dt, reads):
            if name not in debug:
                return
            t = nc.dram_tensor("dbg_" + name, list(shape), dt, kind="ExternalOutput").ap()
            dbg_out[name] = t
            S.dma("sp", t, ap, "dbg_" + name, reads=reads)

        def norm_transpose(stack_bufs, t_glob, dst, dst_keys_fn, gi, xres=None):
            xb, junk, ss, an = stack_bufs
            par = t_glob % 2
            xp = t_glob % len(xb)
            xt = xb[xp]
            kx = ("xt", xp)
            S.dma("sp", xt[:], x_d[t_glob * 128:(t_glob + 1) * 128, :], ("x", xp), writes=[kx])
            S.op("act", lambda e: e.activation(out=junk[:], in_=xt[:], func=AF.Square,
                                               accum_out=ss[:, par:par + 1]),
                 reads=[kx], writes=["junk", ("ss", par)])
            S.op("act", lambda e: e.activation(out=ss[:, par:par + 1], in_=ss[:, par:par + 1], func=AF.Ln,
                                               scale=1.0 / D, bias=eps_col),
                 reads=[("ss", par), "cst"], writes=[("ss", par)])
            S.op("act", lambda e: e.activation(out=ss[:, par:par + 1], in_=ss[:, par:par + 1], func=AF.Exp,
                                               scale=-0.5),
                 reads=[("ss", par)], writes=[("ss", par)])
            if (t_glob // 2) % 2 == 0:
                S.op("act", lambda e: e.activation(out=an[par][:], in_=xt[:], func=AF.Copy, scale=ss[:, par:par + 1]),
                     reads=[kx, ("ss", par)], writes=[("an", par)])
            else:
                S.op("dve", lambda e: e.tensor_scalar(out=an[par][:], in0=xt[:], scalar1=ss[:, par:par + 1],
                                                      scalar2=None, op0=ALU.mult),
                     reads=[kx, ("ss", par)], writes=[("an", par)])
            return xt, kx

        mix = ExitStack()
        es.enter_context(mix)
        aT = sb(mix, "aT", [128, 8, SEQ], BF16)
        aT_keys = [("aT", t) for t in range(NT)]
        qT = sb(mix, "qT", [128, SEQ], BF16)
        kTz = [sb(mix, f"kTz{i}", [128, SEQ], BF16) for i in range(2)]
        vv = sb(mix, "vv", [128, NT, 2, 65], BF16)
        wmx = sb(mix, "wmx", [128, 5120], BF16)
        PT = [sb(mix, f"PT{i}", [128, 512], BF16) for i in range(6)]
        rrow = sb(mix, "rrow", [128, 512], F32)
        S.op("pool", lambda e: e.memset(vv[:], 1.0), writes=["vv"])
        S.op("pool", lambda e: e.memset(rrow[:], 1.0), writes=["rrow"])
        ptr_b = banks[7][:, :].bitcast(BF16)

        def normalise(src65, src_keys, n, dst_ap, dst_key):
            S.op("act", lambda e: e.activation(out=rrow[64:65, 0:n], in_=src65[64:65, :], func=AF.Ln),
                 reads=src_keys, writes=["rrow"])
            S.op("act", lambda e: e.activation(out=rrow[64:65, 0:n], in_=rrow[64:65, 0:n], func=AF.Exp, scale=-1.0),
                 reads=["rrow"], writes=["rrow"])
            S.op("pe", lambda e: e.matmul(banks[7][0:64, 0:n], lhsT=ones32[64:65, 0:64], rhs=rrow[64:65, 0:n],
                                          start=True, stop=True),
                 reads=["rrow", "ones32"], writes=[("ps", 7)])
            S.op("dve", lambda e: e.tensor_tensor(out=dst_ap, in0=src65[0:64, :], in1=banks[7][0:64, 0:n], op=ALU.mult),
                 reads=list(src_keys) + [("ps", 7)], writes=[dst_key])

        with ExitStack() as phd:
            Ctab = sb(phd, "Ctab", [128, SEQ], F32)
            Stab = sb(phd, "Stab", [128, SEQ], F32)
            dmask = sb(phd, "dmask", [128, 3, 128], F32)
            ss = sb(phd, "ss", [128, 2], F32)
            S.dma("sp", dmask[:], dmask_d.rearrange("p (r f) -> p r f", r=3), "c4", writes=["dmask"])
            k0f = kTz[0][:, :].bitcast(F32)
            q0f = qT[:, :].bitcast(F32)
            xb = [k0f[:, 0:1024], k0f[:, 1024:2048], q0f[:, 0:1024], q0f[:, 1024:2048]]
            an = [kTz[1][:, 0:1024], kTz[1][:, 1024:2048]]
            junk = kTz[1][:, 2048:3072]
            wf = wmx[:, :].bitcast(F32)
            wi = wmx[:, :].bitcast(I32)
            ang, kf, rc = wf[:, 0:512], wf[:, 512:1024], wf[:, 1024:1536]
            posi, ki = wi[:, 1536:2048], wi[:, 2048:2560]

            def table_chunk(ch):
                cs_ = slice(ch * 512, (ch + 1) * 512)
                S.dma("pool", posi, pos_d[0:1, cs_].partition_broadcast(128), "posi", writes=["posi"])
                S.op("dve", lambda e: e.tensor_copy(out=ang, in_=posi), reads=["posi"], writes=["ang"])
                S.op("dve", lambda e: e.tensor_scalar(out=ang, in0=ang, scalar1=cst[:, 0:1], scalar2=None,
                                                       op0=ALU.mult), reads=["ang", "cst"], writes=["ang"])
                S.op("dve", lambda e: e.tensor_scalar(out=ki, in0=ang, scalar1=float(1.0 / TWO_PI),
                                                      scalar2=None, op0=ALU.mult), reads=["ang"], writes=["ki"])
                S.op("dve", lambda e: e.tensor_copy(out=kf, in_=ki), reads=["ki"], writes=["kf"])
                S.op("dve", lambda e: e.scalar_tensor_tensor(out=ang, in0=kf, scalar=-CW1, in1=ang,
                                                             op0=ALU.mult, op1=ALU.add),
                     reads=["kf", "ang"], writes=["ang"])
                S.op("dve", lambda e: e.scalar_tensor_tensor(out=ang, in0=kf, scalar=-CW2, in1=ang,
                                                             op0=ALU.mult, op1=ALU.add),
                     reads=["kf", "ang"], writes=["ang"])
                S.op("dve", lambda e: e.tensor_scalar(out=rc, in0=ang, scalar1=float(np.pi / 2), scalar2=None,
                                                       op0=ALU.add), reads=["ang"], writes=["rc"])
                S.op("dve", lambda e: e.tensor_scalar(out=kf, in0=rc, scalar1=float(np.pi),
                                                       scalar2=-TWO_PI, op0=ALU.is_gt, op1=ALU.mult),
                     reads=["rc"], writes=["kf"])
                S.op("dve", lambda e: e.tensor_tensor(out=rc, in0=rc, in1=kf, op=ALU.add),
                     reads=["rc", "kf"], writes=["rc"])
                S.op("dve", lambda e: e.tensor_scalar(out=Ctab[:, cs_], in0=rc, scalar1=PI_SAFE, scalar2=-PI_SAFE,
                                                       op0=ALU.min, op1=ALU.max), reads=["rc"], writes=[("Ctab", ch)])
                S.op("dve", lambda e: e.tensor_scalar(out=Stab[:, cs_], in0=ang, scalar1=PI_SAFE, scalar2=-PI_SAFE,
                                                       op0=ALU.min, op1=ALU.max), reads=["ang"], writes=[("Stab", ch)])

            def table_chunk_sin(ch):
                cs_ = slice(ch * 512, (ch + 1) * 512)
                S.op("act", lambda e: e.activation(out=Stab[:, cs_], in_=ang, func=AF.Sin, scale=cst[:, 1:2]),
                     reads=["ang", "cst"], writes=[("Stab", ch)])
                S.op("act", lambda e: e.activation(out=Ctab[:, cs_], in_=rc, func=AF.Sin),
                     reads=["rc"], writes=[("Ctab", ch)])

            gmix_b = gcols[:, 0:8].unsqueeze(2).broadcast_to([128, 8, 128])
            for t in range(NT):
                par = t % 2
                norm_transpose((xb, junk, ss, an), t, None, None, 0)
                for c in range(8):
                    S.op("pe", lambda e: e.transpose(out=ptr_b[:, c * 128:(c + 1) * 128],
                                                     in_=an[par][:, c * 128:(c + 1) * 128], identity=identb[:]),
                         reads=[("an", par), "identb"], writes=[("ps", 7)], inc=(c == 7))
                S.op("dve", lambda e: e.tensor_tensor(
                    out=aT[:, :, t * 128:(t + 1) * 128],
                    in0=ptr_b.rearrange("p (c j) -> p c j", c=8), in1=gmix_b, op=ALU.mult),
                    reads=[("ps", 7), "gcols"], writes=[("aT", t)])
                if t % 4 == 0:
                    table_chunk(t // 4)
            tkeys = [("Stab", ch) for ch in range(8)] + [("Ctab", ch) for ch in range(8)]
            S.op("act", lambda e: e.activation(out=Stab[:, :], in_=Stab[:, :], func=AF.Sin, scale=cst[:, 1:2]),
                 reads=tkeys + ["cst"], writes=tkeys)
            S.op("act", lambda e: e.activation(out=Ctab[:, :], in_=Ctab[:, :], func=AF.Sin), reads=tkeys, writes=tkeys)
            S.barrier()
            S.op("pool", lambda e: e.memset(kTz[0][:], 0.0), writes=["kT"])
            S.op("pool", lambda e: e.memset(kTz[1][:], 0.0), writes=["kT"])
            debug_dump("aT", aT[:], [128, 8, SEQ], BF16, [])
            debug_dump("Ctab", Ctab[:], [128, SEQ], F32, [])
            debug_dump("Stab", Stab[:], [128, SEQ], F32, [])
            accv = y_naT[:, :, :].rearrange("p a s -> p (a s)").bitcast(F32)
            acc = [accv[0:65, 0:SEQ], accv[0:65, SEQ:2 * SEQ]]
            t1 = [sb(phd, "t1_0", [128, 512], F32)] * 2
            t2 = [sb(phd, "t2_0", [128, 512], F32)] * 2

            def segs(d, n):
                L = SEQ // d
                if L >= 512:
                    r, m0 = (n * 512) // L, (n * 512) % L
                    return [(0, 512, r + d * m0, d)]
                out = []
                per = 512 // L
                for i in range(per):
                    out.append((i * L, L, n * per + i, d))
                return out

            def tsl(start, cnt, step):
                return slice(start, start + step * (cnt - 1) + 1, step)

            for P in range(2):
                for g in range(3):
                    d = DILS[g]
                    L = SEQ // d
                    bps = L // 128
                    pi = 2 * g + P
                    S.dma("pool", wmx[:, 0:5120], wD_d[pi], "wmx", writes=["wmx"])
                    wv = wmx[:, 0:5120].rearrange("p (k n) -> p k n", k=8)
                    for n in range(8):
                        nsl = slice(n * 512, (n + 1) * 512)
                        for which, dkey in ((0, "qT"), (1, "kT")):
                            bq, kq = bank([0, 1, 2, 3, 4, 5])
                            bs, ks = bank([0, 1, 2, 3, 4, 5])
                            for (bk, kk, off) in ((bq, kq, which * 256), (bs, ks, which * 256 + 128)):
                                for kc in range(8):
                                    S.op("pe", lambda e: e.matmul(
                                        bk[:, :], lhsT=wv[:, kc, off:off + 128], rhs=aT[:, kc, nsl],
                                        start=(kc == 0), stop=(kc == 7)),
                                        reads=["wmx"] + aT_keys, writes=[kk], inc=(kc == 7))
                            tp = 0
                            S.op("dve", lambda e: e.tensor_tensor(out=t1[tp][:], in0=bq[:, :], in1=Ctab[:, nsl], op=ALU.mult),
                                 reads=[kq], writes=[("t1", tp)])
                            S.op("act", lambda e: e.copy(out=t2[tp][:], in_=bs[:, :]), reads=[ks], writes=[("t2", tp)])
                            S.op("pool", lambda e: e.tensor_tensor(out=t2[tp][:], in0=t2[tp][:], in1=Stab[:, nsl], op=ALU.mult),
                                 reads=[("t2", tp)], writes=[("t2", tp)])
                            if which == 0:
                                parts = [(slice(0, 128), qT)]
                            else:
                                parts = [(slice(0, 64), kTz[0]), (slice(64, 128), kTz[1])]
                            for (psl, dstT) in parts:
                                if d == 1:
                                    S.op("dve", lambda e: e.tensor_tensor(out=dstT[psl, nsl], in0=t1[tp][psl, :],
                                                                          in1=t2[tp][psl, :], op=ALU.add),
                                         reads=[("t1", tp), ("t2", tp)], writes=[dkey])
                                else:
                                    w_ = 512 // d
                                    S.op("dve", lambda e: e.tensor_tensor(
                                        out=dstT[psl, :].rearrange("p (r l) -> p r l", r=d)[:, :, n * w_:(n + 1) * w_],
                                        in0=t1[tp][psl, :].rearrange("p (j r) -> p r j", r=d),
                                        in1=t2[tp][psl, :].rearrange("p (j r) -> p r j", r=d), op=ALU.add),
                                        reads=[("t1", tp), ("t2", tp)], writes=[dkey])
                    for nb0 in range(0, NT, 4):
                        bv, kv = bank([0, 1, 2, 3, 4, 5])
                        for i in range(4):
                            nb = nb0 + i
                            r, m0 = nb // bps, (nb % bps) * 128
                            sl = tsl(r + d * m0, 128, d)
                            for kc in range(8):
                                S.op("pe", lambda e: e.matmul(bv[:, i * 128:(i + 1) * 128], lhsT=aT[:, kc, sl],
                                                              rhs=wv[:, kc, 512:640], start=(kc == 0), stop=(kc == 7)),
                                     reads=["wmx"] + aT_keys, writes=[kv], inc=(i == 3 and kc == 7))
                        S.op("act", lambda e: e.copy(out=vv[:, nb0:nb0 + 4, :, 0:64],
                                                     in_=bv[:, :].rearrange("p (a h j) -> p a h j", a=4, h=2)),
                             reads=[kv], writes=[("vv", nb0 // 4)])
                    qkeys = ["qT"]
                    kkeys = ["kT"]
                    vkeys = [("vv", i) for i in range(8)] + ["vv"]
                    for hi in range(2):
                        hb = 64 * hi

                        def emit_qk(s4):
                            res = []
                            for rel in (-1, 0, 1):
                                bsc, ksc = bank([0, 1, 2, 3, 4, 5])
                                valid = []
                                for s in range(4):
                                    qb = 4 * s4 + s
                                    kb = qb + rel
                                    ok = 0 <= kb < NT and kb // bps == qb // bps
                                    valid.append(ok)
                                    if ok:
                                        S.op("pe", lambda e: e.matmul(
                                            bsc[:, s * 128:(s + 1) * 128], lhsT=kTz[hi][:, kb * 128:(kb + 1) * 128],
                                            rhs=qT[:, qb * 128:(qb + 1) * 128], start=True, stop=True),
                                            reads=qkeys + kkeys, writes=[ksc])
                                res.append((rel, bsc, ksc, valid))
                            return res

                        cur = emit_qk(0)
                        acc_pend = []
                        for s4 in range(8):
                            pts = []
                            for (rel, bsc, ksc, valid) in cur:
                                if not any(valid):
                                    pts.append(None)
                                    continue
                                pt_i = rot["pt"] % 6
                                rot["pt"] += 1
                                S.op("dve", lambda e: e.scalar_tensor_tensor(
                                    out=bsc[:, :].rearrange("p (a f) -> p a f", a=4),
                                    in0=bsc[:, :].rearrange("p (a f) -> p a f", a=4), scalar=0.125,
                                    in1=dmask[:, rel + 1, :].unsqueeze(1).broadcast_to([128, 4, 128]),
                                    op0=ALU.mult, op1=ALU.add), reads=[ksc, "dmask"], writes=[ksc])
                                S.op("act", lambda e: e.activation(out=PT[pt_i][:], in_=bsc[:, :], func=AF.Exp),
                                     reads=[ksc], writes=[("PT", pt_i)])
                                pts.append(pt_i)
                            nxt = emit_qk(s4 + 1) if s4 + 1 < 8 else None
                            if acc_pend:
                                acc_pend.pop(0)()
                            ob = 6 + (s4 % 2)
                            for s in range(4):
                                qb = 4 * s4 + s
                                use = [(ri, rel) for ri, (rel, _, _, valid) in enumerate(cur) if valid[s]]
                                for idx, (ri, rel) in enumerate(use):
                                    kb = qb + rel
                                    S.op("pe", lambda e: e.matmul(
                                        banks[ob][0:65, s * 128:(s + 1) * 128], lhsT=vv[:, kb, hi, 0:65],
                                        rhs=PT[pts[ri]][:, s * 128:(s + 1) * 128],
                                        start=(idx == 0), stop=(idx == len(use) - 1)),
                                        reads=vkeys + [("PT", pts[ri])], writes=[("ps", ob)],
                                        inc=(s == 3 and idx == len(use) - 1))

                            def acc_update(s4=s4, ob=ob):
                                for (c0, ncol, tstart, tstep) in segs(d, s4):
                                    sl = tsl(tstart, ncol, tstep)
                                    if g == 0:
                                        S.op("act", lambda e: e.copy(out=acc[hi][:, sl], in_=banks[ob][0:65, c0:c0 + ncol]),
                                             reads=[("ps", ob)], writes=[("acc", hi)])
                                    else:
                                        S.op("dve", lambda e: e.tensor_tensor(out=acc[hi][:, sl],
                                                                              in0=banks[ob][0:65, c0:c0 + ncol],
                                                                              in1=acc[hi][:, sl], op=ALU.add),
                                             reads=[("ps", ob), ("acc", hi)], writes=[("acc", hi)])
                            acc_pend.append(acc_update)
                            cur = nxt
                        while acc_pend:
                            acc_pend.pop(0)()
                for hi in range(2):
                    S.op("act", lambda e: e.activation(out=acc[hi][64:65, :], in_=acc[hi][64:65, :], func=AF.Ln),
                         reads=[("acc", hi)], writes=[("acc", hi)])
                    S.op("act", lambda e: e.activation(out=acc[hi][64:65, :], in_=acc[hi][64:65, :], func=AF.Exp,
                                                       scale=-1.0),
                         reads=[("acc", hi)], writes=[("acc", hi)])
                    for n in range(8):
                        sl = slice(n * 512, (n + 1) * 512)
                        bb, kb_ = bank([6, 7])
                        S.op("pe", lambda e: e.matmul(bb[0:64, :], lhsT=ones32[64:65, 0:64], rhs=acc[hi][64:65, sl],
                                                      start=True, stop=True),
                             reads=[("acc", hi), "ones32"], writes=[kb_])
                        S.op("dve", lambda e: e.tensor_tensor(out=y_dilT[64 * hi:64 * hi + 64, P, sl],
                                                              in0=acc[hi][0:64, sl], in1=bb[0:64, :], op=ALU.mult),
                             reads=[("acc", hi), kb_], writes=[("y_dilT", P, hi, n)])
            S.barrier()
        debug_dump("y_dilT", y_dilT[:], [128, 2, SEQ], BF16, [])
        if stop_after == "phd":
            S.finish(["dbg_" + k for k in dbg_out])
            return nc, dbg_out

        with ExitStack() as phn:
            U = [sb(phn, f"U{i}", [128, 2, 1408], F32) for i in range(2)]
            osb = sb(phn, "osb", [65, SEQ], F32)
            qtiles = [(0, 4, 1, list(range(0, 4))), (4, 4, 0, list(range(0, 6)))]
            for i in range(1, 7):
                qtiles.append((8 * i, 8, 0, list(range(4 * i - 2, 4 * i + 6))))
            qtiles += [(56, 4, 0, list(range(26, 32))), (60, 4, 1, list(range(28, 32)))]
            pending_bulk = []
            for pi in range(4):
                S.dma("pool", wmx[:, 0:3072], wN_d[pi], "wmx", writes=["wmx"])
                wv = wmx[:, 0:3072].rearrange("p (k n) -> p k n", k=8)
                for n in range(8):
                    for which in (0, 1):
                        bq, kq = bank([0, 1, 2, 3, 4, 5])
                        for kc in range(8):
                            S.op("pe", lambda e: e.matmul(bq[:, :], lhsT=wv[:, kc, which * 128:(which + 1) * 128],
                                                          rhs=aT[:, kc, n * 512:(n + 1) * 512],
                                                          start=(kc == 0), stop=(kc == 7)),
                                 reads=["wmx"] + aT_keys, writes=[kq], inc=(kc == 7))
                        nsl = slice(n * 512, (n + 1) * 512)
                        if which == 0:
                            S.op("act", lambda e: e.copy(out=qT[:, nsl], in_=bq[:, :]), reads=[kq], writes=[("qT", n)])
                        else:
                            S.op("dve", lambda e: e.tensor_copy(out=kTz[0][0:64, nsl], in_=bq[0:64, :]),
                                 reads=[kq], writes=[("kT", n, 0)])
                            S.op("pool" if False else "act", lambda e: e.copy(out=kTz[1][64:128, nsl], in_=bq[64:128, :]),
                                 reads=[kq], writes=[("kT", n, 1)])
                for nb0 in range(0, NT, 4):
                    bv, kv = bank([0, 1, 2, 3, 4, 5])
                    for i in range(4):
                        nb = nb0 + i
                        for kc in range(8):
                            S.op("pe", lambda e: e.matmul(bv[:, i * 128:(i + 1) * 128],
                                                          lhsT=aT[:, kc, nb * 128:(nb + 1) * 128],
                                                          rhs=wv[:, kc, 256:384], start=(kc == 0), stop=(kc == 7)),
                                 reads=["wmx"] + aT_keys, writes=[kv], inc=(i == 3 and kc == 7))
                    S.op("act", lambda e: e.copy(out=vv[:, nb0:nb0 + 4, :, 0:64],
                                                 in_=bv[:, :].rearrange("p (a h j) -> p a h j", a=4, h=2)),
                         reads=[kv], writes=[("vv", nb0 // 4)])
                qkeys = [("qT", n) for n in range(8)]
                kkeys = [("kT", n, i) for n in range(8) for i in range(2)] + ["kT"]
                vkeys = [("vv", i) for i in range(8)] + ["vv"]
                for hi in range(2):
                    h = 2 * pi + hi
                    hb = 64 * hi
                    Uh = U[h % 2]
                    S.dma("sp", Uh[:], rpbx_d[h].rearrange("t p f -> p t f"), ("U", h % 2), writes=[("U", h % 2)])
                    blocks = []
                    for qi, (R, nr, tsel, jl) in enumerate(qtiles):
                        rng = {}
                        for j in jl:
                            if tsel == 1:
                                rng[j] = (0, nr)
                            else:
                                dlt = 2 * j - R
                                lo, hi_ = max(0, dlt - 3), min(nr - 1, dlt + 5)
                                assert lo <= hi_
                                rng[j] = (lo, hi_ - lo + 1)
                        full = [j for j in jl if rng[j] == (0, nr)]
                        assert full
                        order = [full[0]] + [j for j in jl if j != full[0]]
                        for j in order:
                            blocks.append((qi, R, nr, tsel, j, j == order[0], j == order[-1], rng[j][0], rng[j][1]))

                    def emit_qk(bl):
                        qi, R, nr, tsel, j, first, last, blo, nb = bl
                        n = 64 * nb
                        q0 = 64 * (R + blo)
                        bsc, ksc = bank([0, 1, 2, 3, 6, 7])
                        S.op("pe", lambda e: e.matmul(bsc[:, 0:n], lhsT=kTz[hi][:, j * 128:(j + 1) * 128],
                                                      rhs=qT[:, q0:q0 + n], start=True, stop=True),
                             reads=qkeys + kkeys, writes=[ksc])
                        return bsc, ksc

                    LA = 4
                    pending = []
                    inflight = [emit_qk(b_) for b_ in blocks[:LA]]
                    while pending_bulk:
                        pending_bulk.pop(0)()
                    for bi, bl in enumerate(blocks):
                        qi, R, nr, tsel, j, first, last, blo, nb = bl
                        n = 64 * nb
                        bsc, ksc = inflight.pop(0)
                        m0 = 10 - (2 * j - R) + blo
                        pt_i = rot["pt"] % 6
                        rot["pt"] += 1
                        S.op("dve", lambda e: e.scalar_tensor_tensor(
                            out=bsc[:, 0:n], in0=bsc[:, 0:n], scalar=0.125,
                            in1=Uh[:, tsel, m0 * 64:(m0 + nb) * 64], op0=ALU.mult, op1=ALU.add),
                            reads=[ksc, ("U", h % 2)], writes=[ksc])
                        S.op("act", lambda e: e.activation(out=PT[pt_i][:, 0:n], in_=bsc[:, 0:n], func=AF.Exp),
                             reads=[ksc], writes=[("PT", pt_i)])
                        if bi + LA < len(blocks):
                            inflight.append(emit_qk(blocks[bi + LA]))
                        ob = 4 + (qi % 2)
                        S.op("pe", lambda e: e.matmul(banks[ob][0:65, 64 * blo:64 * blo + n], lhsT=vv[:, j, hi, 0:65],
                                                      rhs=PT[pt_i][:, 0:n], start=first, stop=last),
                             reads=vkeys + [("PT", pt_i)], writes=[("ps", ob)], inc=last)
                        if KEEP_WARM:
                            S.op("pe", lambda e: e.matmul(banks[7][:, :], lhsT=identb[:, :], rhs=aT[:, 0, 0:512],
                                                          start=True, stop=True),
                                 reads=[], writes=[], inc=False)
                        n = 64 * nr
                        if last:
                            def fin(qi=qi, n=n, ob=ob, R=R):
                                tsl_ = slice(64 * R, 64 * R + n)
                                if False:
                                    S.op("act", lambda e: e.copy(out=osb[:, tsl_], in_=banks[ob][0:65, 0:n]),
                                         reads=[("ps", ob)], writes=[("osb", qi)])
                                else:
                                    S.op("dve", lambda e: e.tensor_copy(out=osb[:, tsl_], in_=banks[ob][0:65, 0:n]),
                                         reads=[("ps", ob)], writes=[("osb", qi)])
                                S.op("act", lambda e: e.activation(out=osb[64:65, tsl_], in_=osb[64:65, tsl_], func=AF.Ln),
                                     reads=[("osb", qi)], writes=[("osb", qi)])
                                S.op("act", lambda e: e.activation(out=osb[64:65, tsl_], in_=osb[64:65, tsl_], func=AF.Exp,
                                                                   scale=-1.0),
                                     reads=[("osb", qi)], writes=[("osb", qi)])
                            pending.append((bi + 3, fin))
                        while pending and (pending[0][0] <= bi or bi == len(blocks) - 1):
                            pending.pop(0)[1]()
                    def bulk(hb=hb, pi=pi, hi=hi):
                        okeys = [("osb", qi) for qi in range(len(qtiles))]
                        for n8 in range(8):
                            nsl = slice(n8 * 512, (n8 + 1) * 512)
                            bb, kb_ = bank([4, 5])
                            S.op("pe", lambda e: e.matmul(bb[0:64, :], lhsT=ones32[64:65, 0:64], rhs=osb[64:65, nsl],
                                                          start=True, stop=True),
                                 reads=okeys + ["ones32"], writes=[kb_])
                            S.op("dve", lambda e: e.tensor_tensor(out=y_naT[hb:hb + 64, pi, nsl], in0=osb[0:64, nsl],
                                                                  in1=bb[0:64, :], op=ALU.mult),
                                 reads=okeys + [kb_], writes=[("y_naT", pi, hi, n8)])
                    pending_bulk.append(bulk)
            while pending_bulk:
                pending_bulk.pop(0)()
            S.barrier()
        debug_dump("y_naT", y_naT[:], [128, 4, SEQ], BF16, [])
        mix.close()
        if stop_after == "phn":
            S.finish(["dbg_" + k for k in dbg_out])
            return nc, dbg_out

        post = ExitStack()
        es.enter_context(post)
        wA = sb(post, "wA", [128, 4, 1024], BF16)
        wB = sb(post, "wB", [128, 2, 1024], BF16)
        wPP = sb(post, "wPP", [128, 2, 1024], BF16)
        S.dma("pool", wA[:], wA_d.rearrange("p (k n) -> p k n", k=4), "wA", writes=["wA"])
        S.dma("pool", wB[:], wB_d.rearrange("p (k n) -> p k n", k=2), "wB", writes=["wB"])
        S.dma("pool", wPP[:], wPP_d.rearrange("p (k n) -> p k n", k=2), "wPP", writes=["wPP"])
        NSLOT = 4
        slots = [sb(post, f"slot{i}", [128, 4096], BF16) for i in range(NSLOT)]
        xb = [sb(post, f"pxb{i}", [128, D], F32) for i in range(2)]
        junk = sb(post, "pjunk", [128, D], BF16)
        ss = sb(post, "pss", [128, 2], F32)
        an = [sb(post, f"pan{i}", [128, D], BF16) for i in range(2)]
        pb = [sb(post, f"ppb{i}", [128, 256], F32) for i in range(2)]
        pbb = [sb(post, f"ppbb{i}", [128, 256], BF16) for i in range(2)]
        aTg = sb(post, "aTg", [128, 8, 512], BF16)
        mxT = sb(post, "mxT", [128, 8, 512], BF16)
        pT = [sb(post, f"pT{i}", [128, 2, 512], BF16) for i in range(2)]
        hT = sb(post, "hT", [128, 8, 512], F32)
        uT = sb(post, "uT", [128, 32, 512], BF16)
        sg = [sb(post, f"sg{i}", [128, 512], F32) for i in range(2)]
        mm_ = [sb(post, f"mm{i}", [128, 512], F32) for i in range(2)]
        sq = [sb(post, f"sq{i}", [128, 512], BF16) for i in range(2)]
        onesb = sb(post, "onesb", [128, 128], BF16)
        S.op("pool", lambda e: e.memset(onesb[:], 1.0), writes=["onesb"])
        rbc = sb(post, "rbc", [128, 512], F32)
        ost = [sb(post, f"ost{i}", [128, D], F32) for i in range(2)]
        load_i = {"n": 0}
        POOLB = [0, 1, 2, 3, 4, 5, 6]

        def issue_load(l_glob):
            s = l_glob % NSLOT
            S.dma("pool", slots[s][:], wS_d[l_glob % N_LOADS], ("slot", s), writes=[("slot", s)])

        total_loads = 8 * N_LOADS
        for l in range(NSLOT):
            issue_load(l)
        load_i["n"] = NSLOT

        def next_slot(l_glob):
            return slots[l_glob % NSLOT], ("slot", l_glob % NSLOT)

        def done_slot():
            if load_i["n"] < total_loads:
                issue_load(load_i["n"])
                load_i["n"] += 1

        POOLB[:] = [0, 1, 2, 3, 4, 5]

        def sum_sq(c, first, last):
            S.op("act", lambda e: e.activation(out=sq[c % 2][:], in_=hT[:, c, :], func=AF.Square),
                 reads=[("hT", c)], writes=[("sq", c % 2)])
            S.op("pe", lambda e: e.matmul(banks[6][:, :], lhsT=onesb[:, :], rhs=sq[c % 2][:], start=first, stop=last),
                 reads=[("sq", c % 2), "onesb"], writes=[("ps", 6)])

        def rms_finish(gi, dstT, dkey):
            S.op("act", lambda e: e.activation(out=rbc[:], in_=banks[6][:, :], func=AF.Ln, scale=1.0 / D, bias=eps_col),
                 reads=[("ps", 6), "cst"], writes=["rbc"])
            S.op("act", lambda e: e.activation(out=rbc[:], in_=rbc[:], func=AF.Exp, scale=-0.5),
                 reads=["rbc"], writes=["rbc"])
            for c in range(8):
                S.op("dve", lambda e: e.scalar_tensor_tensor(out=dstT[:, c, :], in0=hT[:, c, :],
                                                              scalar=gcols[:, gi * 8 + c:gi * 8 + c + 1], in1=rbc[:],
                                                              op0=ALU.mult, op1=ALU.mult),
                     reads=[("hT", c), "rbc", "gcols"], writes=[(dkey, c)])

        def head_A_p1(grp, tt):
            t = grp * 4 + tt
            par = t % 2
            norm_transpose((xb, junk, ss, an), t, None, None, 0)
            S.dma("sp", pb[par][:], p_d[t * 128:(t + 1) * 128, :], ("pb", par), writes=[("pb", par)])
            S.op("act", lambda e: e.copy(out=pbb[par][:], in_=pb[par][:]), reads=[("pb", par)], writes=[("pbb", par)])

        def head_A_p2(grp, tt):
            t = grp * 4 + tt
            par = t % 2
            pTg = pT[grp % 2]
            for c in range(8):
                S.op("pe", lambda e: e.transpose(out=ptr_b[:, c * 128:(c + 1) * 128],
                                                 in_=an[par][:, c * 128:(c + 1) * 128], identity=identb[:]),
                     reads=[("an", par), "identb"], writes=[("ps", 7)], inc=(c == 7))
            S.op("dve", lambda e: e.tensor_tensor(
                out=aTg[:, :, tt * 128:(tt + 1) * 128], in0=ptr_b.rearrange("p (c j) -> p c j", c=8),
                in1=gcols[:, 0:8].unsqueeze(2).broadcast_to([128, 8, 128]), op=ALU.mult),
                reads=[("ps", 7), "gcols"], writes=[("aTg", c) for c in range(8)])
            for c2 in range(2):
                S.op("pe", lambda e: e.transpose(out=ptr_b[:, c2 * 128:(c2 + 1) * 128],
                                                 in_=pbb[par][:, c2 * 128:(c2 + 1) * 128], identity=identb[:]),
                     reads=[("pbb", par), "identb"], writes=[("ps", 7)], inc=(c2 == 1))
            S.op("act", lambda e: e.copy(out=pTg[:, :, tt * 128:(tt + 1) * 128],
                                         in_=ptr_b[:, 0:256].rearrange("p (c j) -> p c j", c=2)),
                 reads=[("ps", 7)], writes=[("pT", grp % 2)])

        def head_A_tile(grp, tt):
            head_A_p1(grp, tt)
            head_A_p2(grp, tt)

        def head_A(grp):
            for tt in range(4):
                head_A_tile(grp, tt)

        def head_B_p1(grp, tt):
            t = grp * 4 + tt
            par = t % 2
            S.dma("sp", ost[par][:], x_d[t * 128:(t + 1) * 128, :], ("x2", par),
                  writes=[("ost", par, 0), ("ost", par, 1)])

        def head_B_p2(grp, tt):
            t = grp * 4 + tt
            par = t % 2
            for half in range(2):
                bx, kxb = bank(POOLB)
                for c4 in range(4):
                    c = half * 4 + c4
                    S.op("pe", lambda e: e.transpose(out=bx[:, c4 * 128:(c4 + 1) * 128],
                                                     in_=ost[par][:, c * 128:(c + 1) * 128], identity=identf[:]),
                         reads=[("ost", par, half), "identf"], writes=[kxb], inc=(c4 == 3))
                if half == 0:
                    S.op("act", lambda e: e.copy(out=hT[:, 0:4, tt * 128:(tt + 1) * 128],
                                                 in_=bx[:, :].rearrange("p (c j) -> p c j", c=4)),
                         reads=[kxb], writes=[("hT", c4) for c4 in range(4)])
                else:
                    S.op("dve", lambda e: e.tensor_copy(out=hT[:, 4:8, tt * 128:(tt + 1) * 128],
                                                        in_=bx[:, :].rearrange("p (c j) -> p c j", c=4)),
                         reads=[kxb], writes=[("hT", 4 + c4) for c4 in range(4)])

        lgc = {"n": 0}

        def gates_half(grp, half, with_head_b):
            tok0 = grp * 512
            lg = lgc["n"]
            sl_na, k_na = next_slot(lg)
            sl_dl, k_dl = next_slot(lg + 1)
            wna = sl_na[:, :].rearrange("p (o k j) -> p o k j", o=4, k=8)
            wdl = sl_dl[:, :].rearrange("p (o k j) -> p o k j", o=4, k=8)
            for o in range(4):
                c = half * 4 + o
                b1, k1 = bank(POOLB)
                for kc in range(8):
                    S.op("pe", lambda e: e.matmul(b1[:, :], lhsT=wna[:, o, kc, :], rhs=aTg[:, kc, :],
                                                  start=(kc == 0), stop=(kc == 7)),
                         reads=[k_na, ("aTg", kc)], writes=[k1], inc=(kc == 7))
                S.op("act", lambda e: e.activation(out=sg[0][:], in_=b1[:, :], func=AF.Sigmoid),
                     reads=[k1], writes=[("sg", 0)])
                b2, k2 = bank(POOLB)
                for kc in range(8):
                    S.op("pe", lambda e: e.matmul(b2[:, :], lhsT=wdl[:, o, kc, :], rhs=aTg[:, kc, :],
                                                  start=(kc == 0), stop=(kc == 7)),
                         reads=[k_dl, ("aTg", kc)], writes=[k2], inc=(kc == 7))
                S.op("act", lambda e: e.activation(out=sg[1][:], in_=b2[:, :], func=AF.Sigmoid),
                     reads=[k2], writes=[("sg", 1)])
                b3, k3 = bank(POOLB)
                for kc in range(4):
                    S.op("pe", lambda e: e.matmul(b3[:, :], lhsT=wA[:, kc, c * 128:(c + 1) * 128],
                                                  rhs=y_naT[:, kc, tok0:tok0 + 512], start=(kc == 0), stop=(kc == 3)),
                         reads=["wA"], writes=[k3], inc=(kc == 3))
                S.op("dve", lambda e: e.tensor_tensor(out=mm_[0][:], in0=b3[:, :], in1=sg[0][:], op=ALU.mult),
                     reads=[k3, ("sg", 0)], writes=[("mm", 0)])
                b4, k4 = bank(POOLB)
                for kc in range(2):
                    S.op("pe", lambda e: e.matmul(b4[:, :], lhsT=wB[:, kc, c * 128:(c + 1) * 128],
                                                  rhs=y_dilT[:, kc, tok0:tok0 + 512], start=(kc == 0), stop=(kc == 1)),
                         reads=["wB"], writes=[k4], inc=(kc == 1))
                S.op("dve", lambda e: e.tensor_tensor(out=mm_[1][:], in0=b4[:, :], in1=sg[1][:], op=ALU.mult),
                     reads=[k4, ("sg", 1)], writes=[("mm", 1)])
                S.op("pool", lambda e: e.tensor_tensor(out=mxT[:, c, :], in0=mm_[0][:], in1=mm_[1][:], op=ALU.add),
                     reads=[("mm", 0), ("mm", 1)], writes=[("mxT", c)])
                if half == 0 and o == 1 and grp > 0:
                    final_norm()
                if with_head_b:
                    if o == 1:
                        if grp > 0:
                            output_stage(grp - 1)
                        head_B_p1(grp, 0)
                        head_B_p1(grp, 1)
                    elif o == 2:
                        head_B_p2(grp, 0)
                        head_B_p1(grp, 2)
                    elif o == 3:
                        head_B_p2(grp, 1)
                        head_B_p1(grp, 3)
            lgc["n"] += 2
            done_slot()
            done_slot()

        def output_stage(grp):
            for tt in range(4):
                t = grp * 4 + tt
                par = t % 2
                for half in range(2):
                    bx, kxb = bank(POOLB)
                    for c4 in range(4):
                        c = half * 4 + c4
                        S.op("pe", lambda e: e.transpose(out=bx[:, c4 * 128:(c4 + 1) * 128],
                                                         in_=hT[:, c, tt * 128:(tt + 1) * 128], identity=identf[:]),
                             reads=[("hT", c), "identf"], writes=[kxb], inc=(c4 == 3))
                    if half == 0:
                        S.op("act", lambda e: e.copy(out=ost[par][:, 0:512], in_=bx[:, :]),
                             reads=[kxb], writes=[("ost", par, 0)])
                    else:
                        S.op("dve", lambda e: e.tensor_copy(out=ost[par][:, 512:1024], in_=bx[:, :]),
                             reads=[kxb], writes=[("ost", par, 1)])
                S.dma("sp", out_d[t * 128:(t + 1) * 128, :], ost[par][:], ("out", par),
                      reads=[("ost", par, 0), ("ost", par, 1)])

        def final_norm():
            for c in range(8):
                sum_sq(c, c == 0, c == 7)
            rms_finish(3, hT, "hT")

        head_A(0)
        for grp in range(8):
            pTg = pT[grp % 2]
            pkey = ("pT", grp % 2)
            gates_half(grp, 0, False)
            gates_half(grp, 1, True)
            head_B_p2(grp, 2)
            head_B_p2(grp, 3)
            for half in range(2):
                sl, ksl = next_slot(lgc["n"])
                wv_ = sl[:, :].rearrange("p (o k j) -> p o k j", o=4, k=8)
                for o in range(4):
                    c = half * 4 + o
                    b1, k1 = bank(POOLB)
                    for kc in range(8):
                        S.op("pe", lambda e: e.matmul(b1[:, :], lhsT=wv_[:, o, kc, :], rhs=mxT[:, kc, :],
                                                      start=(kc == 0), stop=(kc == 7)),
                             reads=[ksl, ("mxT", kc)], writes=[k1], inc=(kc == 7))
                    if c > 1:
                        sum_sq(c - 2, c == 2, False)
                    S.op("dve", lambda e: e.tensor_tensor(out=hT[:, c, :], in0=b1[:, :], in1=hT[:, c, :], op=ALU.add),
                         reads=[k1, ("hT", c)], writes=[("hT", c)])
                lgc["n"] += 1
                done_slot()
            sum_sq(6, False, False)
            sum_sq(7, False, True)
            rms_finish(1, mxT, "mxT")
            for i in range(8):
                sl, ksl = next_slot(lgc["n"])
                wv_ = sl[:, :].rearrange("p (o k j) -> p o k j", o=4, k=8)
                bks = [bank(POOLB) for _ in range(4)]
                if i == 0:
                    for kc in range(8):
                        for o in range(4):
                            S.op("pe", lambda e: e.matmul(bks[o][0][:, :], lhsT=wv_[:, o, kc, :], rhs=mxT[:, kc, :],
                                                          start=(kc == 0), stop=(kc == 7)),
                                 reads=[ksl, ("mxT", kc)], writes=[bks[o][1]], inc=(kc == 7))
                else:
                    for o in range(4):
                        for kc in range(8):
                            S.op("pe", lambda e: e.matmul(bks[o][0][:, :], lhsT=wv_[:, o, kc, :], rhs=mxT[:, kc, :],
                                                          start=(kc == 0), stop=(kc == 7)),
                                 reads=[ksl, ("mxT", kc)], writes=[bks[o][1]], inc=(kc == 7))
                for o in range(4):
                    fc = 4 * i + o
                    ri = fc % 2
                    S.op("act", lambda e: e.activation(out=sg[ri][:], in_=bks[o][0][:, :], func=AF.Relu),
                         reads=[bks[o][1]], writes=[("sg", ri)])
                    S.op("pool", lambda e: e.tensor_tensor(out=uT[:, fc, :], in0=sg[ri][:], in1=sg[ri][:], op=ALU.mult),
                         reads=[("sg", ri)], writes=[("uT", fc)])
                lgc["n"] += 1
                done_slot()
                if grp + 1 < 8 and i % 2 == 0:
                    if i >= 2:
                        head_A_p2(grp + 1, i // 2 - 1)
                    head_A_p1(grp + 1, i // 2)
            if grp + 1 < 8:
                head_A_p2(grp + 1, 3)
            for c in range(8):
                sl, ksl = next_slot(lgc["n"])
                wv_ = sl[:, :].rearrange("p (k j) -> p k j", k=32)
                b1, k1 = bank(POOLB)
                for kc in range(32):
                    S.op("pe", lambda e: e.matmul(b1[:, :], lhsT=wv_[:, kc, :], rhs=uT[:, kc, :],
                                                  start=(kc == 0), stop=(kc == 31)),
                         reads=[ksl, ("uT", kc)], writes=[k1], inc=(kc == 31))
                if c > 0:
                    sum_sq(c - 1, c == 1, False)
                S.op("dve", lambda e: e.tensor_tensor(out=hT[:, c, :], in0=b1[:, :], in1=hT[:, c, :], op=ALU.add),
                     reads=[k1, ("hT", c)], writes=[("hT", c)])
                lgc["n"] += 1
                done_slot()
            sum_sq(7, False, True)
            rms_finish(2, mxT, "mxT")
            for half in range(2):
                sl, ksl = next_slot(lgc["n"])
                wv_ = sl[:, :].rearrange("p (o k j) -> p o k j", o=4, k=8)
                bks = [bank(POOLB) for _ in range(4)]
                if half == 0:
                    for kc in range(8):
                        for o in range(4):
                            S.op("pe", lambda e: e.matmul(bks[o][0][:, :], lhsT=wv_[:, o, kc, :], rhs=mxT[:, kc, :],
                                                          start=(kc == 0), stop=(kc == 7)),
                                 reads=[ksl, ("mxT", kc)], writes=[bks[o][1]], inc=(kc == 7))
                else:
                    for o in range(4):
                        for kc in range(8):
                            S.op("pe", lambda e: e.matmul(bks[o][0][:, :], lhsT=wv_[:, o, kc, :], rhs=mxT[:, kc, :],
                                                          start=(kc == 0), stop=(kc == 7)),
                                 reads=[ksl, ("mxT", kc)], writes=[bks[o][1]], inc=(kc == 7))
                for o in range(4):
                    c = half * 4 + o
                    S.op("act", lambda e: e.activation(out=sg[o % 2][:], in_=bks[o][0][:, :], func=AF.Sigmoid),
                         reads=[bks[o][1]], writes=[("sg", o % 2)])
                    b2, k2 = bank(POOLB)
                    for kc in range(2):
                        S.op("pe", lambda e: e.matmul(b2[:, :], lhsT=wPP[:, kc, c * 128:(c + 1) * 128], rhs=pTg[:, kc, :],
                                                      start=(kc == 0), stop=(kc == 1)),
                             reads=["wPP", pkey], writes=[k2], inc=(kc == 1))
                    S.op("dve", lambda e: e.tensor_tensor(out=mm_[o % 2][:], in0=b2[:, :], in1=sg[o % 2][:], op=ALU.mult),
                         reads=[k2, ("sg", o % 2)], writes=[("mm", o % 2)])
                    S.op("pool", lambda e: e.tensor_tensor(out=hT[:, c, :], in0=hT[:, c, :], in1=mm_[o % 2][:], op=ALU.add),
                         reads=[("mm", o % 2), ("hT", c)], writes=[("hT", c)])
                lgc["n"] += 1
                done_slot()
        final_norm()
        output_stage(7)
        S.finish([("out", 0), ("out", 1)] + ["dbg_" + k for k in dbg_out])
        post.close()
        print("inst", S.n_inst, "waits", S.n_wait, "nsem", S.nsem, flush=True)
    return nc, dbg_out


def kernel(**inputs):
    sh = _prep_shared(inputs)
    x = np.asarray(inputs["x"], np.float32)
    p = np.asarray(inputs["p"], np.float32)[0]
    pos = np.asarray(inputs["positions"], np.int32)
    nc = build_program()[0]
    in_maps = []
    for b in range(8):
        m = dict(sh)
        m["x"] = np.ascontiguousarray(x[b])
        m["p"] = np.ascontiguousarray(p[b])
        m["pos"] = np.ascontiguousarray(pos[b:b + 1])
        in_maps.append(m)
    res = run_bass_kernel_spmd(nc, in_maps, core_ids=list(range(8)))
    return np.stack([np.asarray(r["out"], np.float32) for r in res.results], axis=0)
```
